# Optimizing a Trainium2 kernel written in Bass

```python
import jax, jax.numpy as jnp
from jax import lax
import numpy as np

D_MODEL = 1024
BATCH = 2
SEQ = 8192
DEPTH = 1

D_MIX = D_MODEL
D_MLSTM = D_MIX // 2
N_MLSTM_HEADS = 4
MLSTM_HEAD_DIM = D_MLSTM // N_MLSTM_HEADS
D_NA = D_MIX - D_MLSTM
N_NA_HEADS = 8
NA_HEAD_DIM = D_NA // N_NA_HEADS
GRID_W = 64
WIN_H_MAX = 8
WIN_W = 16
CHUNK = 128
CONV_W = 3
N_GATE = 4 * N_MLSTM_HEADS
D_FF = 4 * D_MODEL
D_PLE = 256
RMS_EPS = 1e-6
SPLITS = (D_MLSTM, D_MLSTM, D_MLSTM, D_MLSTM, N_GATE, D_NA, D_NA, D_NA)
D_IN_PROJ = sum(SPLITS)

kernel_name = "hybrid_mlstm_natten_block"


def rms_norm(x, g):
    xf = x.astype(jnp.float32)
    y = xf * lax.rsqrt(jnp.mean(xf * xf, axis=-1, keepdims=True) + RMS_EPS)
    return (y * g.astype(jnp.float32)).astype(x.dtype)


def centred_depthwise_conv(x, w, b):
    c = x.shape[-1]
    y = lax.conv_general_dilated(x, w[:, None, :].astype(x.dtype), window_strides=(1,), padding='SAME',
                                 dimension_numbers=('NWC', 'WIO', 'NWC'), feature_group_count=c)
    return y + b.astype(x.dtype)


def mlstm_chunkwise(q, k, v, log_i, log_f):
    B, H, S, d = q.shape
    nc = S // CHUNK

    def to_chunks(a):
        a = a.reshape(B, H, nc, CHUNK, *a.shape[3:])
        return jnp.moveaxis(a, 2, 0)

    f32 = jnp.float32
    xs = (to_chunks(q.astype(f32)), to_chunks(k.astype(f32)), to_chunks(v.astype(f32)),
          to_chunks(log_i), to_chunks(log_f))
    tri = jnp.tril(jnp.ones((CHUNK, CHUNK), dtype=bool))

    def step(carry, inp):
        C, n, m = carry
        qb, kb, vb, ib, fb = inp
        b = jnp.cumsum(fb, axis=-1)
        g = b + m[..., None]
        dmat = jnp.where(tri, b[..., :, None] - b[..., None, :] + ib[..., None, :], -jnp.inf)
        m_t = jnp.maximum(g, jnp.max(dmat, axis=-1))
        w_inter = jnp.exp(g - m_t)
        s = jnp.einsum('bhld,bhsd->bhls', qb, kb) * jnp.exp(dmat - m_t[..., None])
        num = w_inter[..., None] * jnp.einsum('bhld,bhde->bhle', qb, C) + jnp.einsum('bhls,bhse->bhle', s, vb)
        den = w_inter * jnp.einsum('bhld,bhd->bhl', qb, n) + jnp.sum(s, axis=-1)
        h = num / jnp.maximum(jnp.abs(den), jnp.exp(-m_t))[..., None]
        b_last = b[..., -1]
        a_prev = m + b_last
        a_j = ib + b_last[..., None] - b
        m_new = jnp.maximum(a_prev, jnp.max(a_j, axis=-1))
        w_prev = jnp.exp(a_prev - m_new)
        w_j = jnp.exp(a_j - m_new[..., None])
        C_new = w_prev[..., None, None] * C + jnp.einsum('bhs,bhsd,bhse->bhde', w_j, kb, vb)
        n_new = w_prev[..., None] * n + jnp.einsum('bhs,bhsd->bhd', w_j, kb)
        return (C_new, n_new, m_new), h

    init = (jnp.zeros((B, H, d, d), f32), jnp.zeros((B, H, d), f32), jnp.zeros((B, H), f32))
    _, hs = lax.scan(step, init, xs)
    return jnp.moveaxis(hs, 0, 2).reshape(B, H, S, d).astype(q.dtype)


def mlstm_mixer(q_raw, k_raw, v, o_pre, gate_pre, conv_w, conv_b, gate_b, head_norm_g):
    B, S, _ = v.shape
    qk = jax.nn.silu(centred_depthwise_conv(jnp.concatenate([q_raw, k_raw], axis=-1), conv_w, conv_b))
    q, k = jnp.split(qk, 2, axis=-1)

    def heads(a):
        return a.reshape(B, S, N_MLSTM_HEADS, MLSTM_HEAD_DIM).transpose(0, 2, 1, 3)

    q, k, vh = heads(q), heads(k) * (MLSTM_HEAD_DIM ** -0.5), heads(v)
    gates = (gate_pre.astype(jnp.float32) + gate_b.astype(jnp.float32))
    gates = gates.reshape(B, S, 4, N_MLSTM_HEADS).transpose(2, 0, 3, 1)
    i_fwd, f_fwd, i_bwd, f_bwd = gates[0], gates[1], gates[2], gates[3]
    h_fwd = mlstm_chunkwise(q, k, vh, i_fwd, jax.nn.log_sigmoid(f_fwd))
    flip = lambda a: jnp.flip(a, axis=2)
    h_bwd = flip(mlstm_chunkwise(flip(q), flip(k), flip(vh), flip(i_bwd), flip(jax.nn.log_sigmoid(f_bwd))))
    h = (h_fwd + h_bwd).transpose(0, 2, 1, 3)
    h = rms_norm(h, head_norm_g).reshape(B, S, D_MLSTM)
    return h * jax.nn.sigmoid(o_pre)


def neighbourhood_attention(q, k, v, q_norm_g, k_norm_g, rpb):
    B, S, _ = q.shape
    rows = S // GRID_W
    win_h = min(WIN_H_MAX, rows)

    def grid(a):
        return a.reshape(B, rows, GRID_W, N_NA_HEADS, NA_HEAD_DIM).transpose(0, 3, 1, 2, 4)

    split_heads = lambda a: a.reshape(B, S, N_NA_HEADS, NA_HEAD_DIM)
    qg = grid(rms_norm(split_heads(q), q_norm_g) * (NA_HEAD_DIM ** -0.5))
    kg = grid(rms_norm(split_heads(k), k_norm_g))
    vg = grid(split_heads(v))

    cols = jnp.arange(GRID_W)
    col_start = jnp.clip(cols - WIN_W // 2, 0, GRID_W - WIN_W)
    col_idx = col_start[:, None] + jnp.arange(WIN_W)
    col_off = col_idx - cols[:, None] + (WIN_W - 1)

    def one_row(r):
        rs = jnp.clip(r - win_h // 2, 0, rows - win_h)
        k_win = lax.dynamic_slice_in_dim(kg, rs, win_h, axis=2)[:, :, :, col_idx]
        v_win = lax.dynamic_slice_in_dim(vg, rs, win_h, axis=2)[:, :, :, col_idx]
        q_row = lax.dynamic_index_in_dim(qg, r, axis=2, keepdims=False)
        s = jnp.einsum('bhcd,bhrcwd->bhcrw', q_row, k_win).astype(jnp.float32)
        row_off = rs + jnp.arange(win_h) - r + (WIN_H_MAX - 1)
        bias = rpb[:, row_off][:, :, col_off].astype(jnp.float32)
        s = s + bias.transpose(0, 2, 1, 3)[None]
        prob = jax.nn.softmax(s.reshape(B, N_NA_HEADS, GRID_W, win_h * WIN_W), axis=-1)
        prob = prob.reshape(B, N_NA_HEADS, GRID_W, win_h, WIN_W).astype(v.dtype)
        return jnp.einsum('bhcrw,bhrcwd->bhcd', prob, v_win)

    out = lax.map(one_row, jnp.arange(rows))
    return out.transpose(1, 0, 3, 2, 4).reshape(B, S, D_NA)


def setup_inputs(seed: int = 0) -> dict:
    key = jax.random.key(seed)
    ks = jax.random.split(key, 20)
    nrm = lambda k, shape: jax.random.normal(k, shape, dtype=jnp.float32)
    H = N_MLSTM_HEADS
    f_bias = jnp.tile(jnp.linspace(3.0, 6.0, H, dtype=jnp.float32), 2)
    gate_base = jnp.stack([jnp.zeros((H,), jnp.float32), f_bias[:H], jnp.zeros((H,), jnp.float32), f_bias[H:]]).reshape(-1)
    return {
        "x": nrm(ks[0], (BATCH, SEQ, D_MODEL)),
        "p": nrm(ks[1], (DEPTH, BATCH, SEQ, D_PLE)),
        "norm1_g": 1.0 + 0.01 * nrm(ks[2], (DEPTH, D_MODEL)),
        "w_in": nrm(ks[3], (DEPTH, D_MODEL, D_IN_PROJ)) * D_MODEL ** -0.5,
        "conv_w": nrm(ks[4], (DEPTH, CONV_W, 2 * D_MLSTM)) * CONV_W ** -0.5,
        "conv_b": 0.01 * nrm(ks[5], (DEPTH, 2 * D_MLSTM)),
        "gate_b": gate_base[None] + 0.1 * nrm(ks[6], (DEPTH, N_GATE)),
        "mlstm_norm_g": 1.0 + 0.01 * nrm(ks[7], (DEPTH, N_MLSTM_HEADS, MLSTM_HEAD_DIM)),
        "q_norm_g": 1.0 + 0.01 * nrm(ks[8], (DEPTH, N_NA_HEADS, NA_HEAD_DIM)),
        "k_norm_g": 1.0 + 0.01 * nrm(ks[9], (DEPTH, N_NA_HEADS, NA_HEAD_DIM)),
        "rpb": 0.02 * nrm(ks[10], (DEPTH, N_NA_HEADS, 2 * WIN_H_MAX - 1, 2 * WIN_W - 1)),
        "w_out": nrm(ks[11], (DEPTH, D_MIX, D_MODEL)) * D_MIX ** -0.5,
        "norm2_g": 1.0 + 0.01 * nrm(ks[12], (DEPTH, D_MODEL)),
        "w_ff1": nrm(ks[13], (DEPTH, D_MODEL, D_FF)) * D_MODEL ** -0.5,
        "w_ff2": nrm(ks[14], (DEPTH, D_FF, D_MODEL)) * D_FF ** -0.5,
        "ple_norm_g": 1.0 + 0.01 * nrm(ks[15], (DEPTH, D_MODEL)),
        "w_ple_gate": nrm(ks[16], (DEPTH, D_MODEL, D_MODEL)) * D_MODEL ** -0.5,
        "w_ple_up": nrm(ks[17], (DEPTH, D_PLE, D_MODEL)) * D_PLE ** -0.5,
    }


def reference(x, p, norm1_g, w_in, conv_w, conv_b, gate_b, mlstm_norm_g, q_norm_g, k_norm_g, rpb,
              w_out, norm2_g, w_ff1, w_ff2, ple_norm_g, w_ple_gate, w_ple_up):
    h = x
    offsets = list(np.cumsum(SPLITS)[:-1])
    for i in range(DEPTH):
        u = rms_norm(h, norm1_g[i])
        proj = jnp.einsum('bsd,de->bse', u, w_in[i])
        mq, mk, mv, mo, mg, nq, nk, nv = jnp.split(proj, offsets, axis=-1)
        y_a = mlstm_mixer(mq, mk, mv, mo, mg, conv_w[i], conv_b[i], gate_b[i], mlstm_norm_g[i])
        y_b = neighbourhood_attention(nq, nk, nv, q_norm_g[i], k_norm_g[i], rpb[i])
        h = h + jnp.einsum('bse,ed->bsd', jnp.concatenate([y_a, y_b], axis=-1), w_out[i])
        z = jnp.einsum('bsd,df->bsf', rms_norm(h, norm2_g[i]), w_ff1[i])
        h = h + jnp.einsum('bsf,fd->bsd', jnp.square(jax.nn.relu(z)), w_ff2[i])
        gate = jax.nn.sigmoid(jnp.einsum('bsd,de->bse', rms_norm(h, ple_norm_g[i]), w_ple_gate[i]))
        h = h + gate * jnp.einsum('bsk,kd->bsd', p[i], w_ple_up[i])
    return h
```

```python
import contextlib
import os
NA_PARTS = os.environ.get('NA_PARTS', 's,exp,mul,pv,norm').split(',')
XCHG = os.environ.get('XCHG', '1') == '1'
MSUB = float(os.environ.get('MSUB', '9'))
LPART = int(os.environ.get('LPART', '9'))
LSKIP = os.environ.get('LSKIP', '').split(',')
import numpy as np
import concourse.bass as bass
import concourse.mybir as mybir
from concourse.bass_utils import run_bass_kernel_spmd

F32 = mybir.dt.float32
BF16 = mybir.dt.bfloat16
ALU = mybir.AluOpType
AF = mybir.ActivationFunctionType
AX = mybir.AxisListType

NCORES = 8
D = 1024
TOK = 2048
NT = 16
HALO = 256
EXT = TOK + 2 * HALO
NE = EXT // 128
DIN = 3600
NEG = -30000.0
EPS = 1e-6

ENGS = ("pe", "act", "dve", "pool", "sp")
NDSEM = 20
NDSEM_HW = 12


class Res:
    __slots__ = ("name", "lw", "rd")

    def __init__(self, name=""):
        self.name = name
        self.lw = None
        self.rd = []


class Op:
    __slots__ = ("eng", "fn", "deps", "dma", "sig", "sem", "target", "inc", "cc")


class Prog:
    def __init__(self):
        self.ops = []
        self.lastc = {e: None for e in ENGS}
        self.bar = {e: [] for e in ENGS}
        self.dma_since = []

    def add(self, eng, fn, r=(), w=(), dma=False, cc=False):
        idx = len(self.ops)
        deps = set()
        raw = set()
        for x in r:
            if x.lw is not None:
                deps.add(x.lw)
                raw.add(x.lw)
        for x in w:
            if x.lw is not None:
                deps.add(x.lw)
            deps.update(x.rd)
        if not dma and not cc:
            deps = {d for d in deps if d in raw or self.ops[d].dma or self.ops[d].eng != eng}
        if self.bar[eng]:
            deps.update(self.bar[eng])
            self.bar[eng] = []
        op = Op()
        op.eng, op.fn, op.deps, op.dma = eng, fn, deps, dma
        op.sig, op.sem, op.target, op.inc = False, None, 0, 1
        op.cc = cc
        if cc:
            op.dma = True
        self.ops.append(op)
        for x in r:
            x.rd.append(idx)
        for x in w:
            x.lw = idx
            x.rd = []
        if dma or cc:
            self.dma_since.append(idx)
        else:
            self.lastc[eng] = idx
        return idx

    def barrier(self):
        deps = [v for v in self.lastc.values() if v is not None] + list(self.dma_since)
        for e in ENGS:
            self.bar[e] = list(set(self.bar[e]) | set(deps))
        self.dma_since = []

    def emit(self, nc, stack):
        ops = self.ops
        sems = {e: nc.alloc_semaphore(name="s_" + e) for e in ENGS}
        dsems = [nc.alloc_semaphore(name="s_d%d" % i) for i in range(NDSEM)]
        duse = [0] * NDSEM
        dlast = [None] * NDSEM
        k = 0
        kp = 0
        ccsem = nc.alloc_semaphore(name="s_cc")
        for i, op in enumerate(ops):
            if op.cc:
                op.sem, op.target, op.inc, op.sig = ccsem, 1, None, True
                continue
            if op.dma:
                if op.eng == "pool":
                    s = NDSEM_HW + kp % (NDSEM - NDSEM_HW)
                    kp += 1
                else:
                    s = k % NDSEM_HW
                    k += 1
                if dlast[s] is not None:
                    op.deps.add(dlast[s])
                dlast[s] = i
                duse[s] += 1
                op.sem, op.target, op.inc, op.sig = dsems[s], 16 * duse[s], 16, True
        for i, op in enumerate(ops):
            for d in op.deps:
                dop = ops[d]
                if dop.dma:
                    continue
                if dop.eng == "pe" and op.eng == "pe" and not op.dma:
                    continue
                dop.sig = True
        cnt = {e: 0 for e in ENGS}
        for op in ops:
            if op.sig and not op.dma:
                cnt[op.eng] += 1
                op.sem, op.target, op.inc = sems[op.eng], cnt[op.eng], 1
        streams = {e: [op for op in ops if op.eng == e] for e in ENGS}
        semkey = {id(s): s for s in list(sems.values()) + dsems + [ccsem]}
        nwaits = [0]

        def body_for(ename):
            def body(e):
                known = {}
                for op in streams[ename]:
                    waits = {}
                    for d in op.deps:
                        dop = ops[d]
                        if (not dop.dma) and dop.eng == "pe" and ename == "pe" and not op.dma:
                            continue
                        key = id(dop.sem)
                        if waits.get(key, 0) < dop.target:
                            waits[key] = dop.target
                    for key, t in waits.items():
                        if known.get(key, 0) >= t:
                            continue
                        e.wait_ge(semkey[key], t)
                        nwaits[0] += 1
                        known[key] = t
                    ins = op.fn(e)
                    if op.sig:
                        if op.inc is None:
                            ins.then_inc(op.sem)
                        else:
                            ins.then_inc(op.sem, op.inc)
                if ename == "sp":
                    for s in range(NDSEM):
                        if duse[s] and known.get(id(dsems[s]), 0) < 16 * duse[s]:
                            e.wait_ge(dsems[s], 16 * duse[s])
            return body

        with nc.Block() as block:
            block.tensor(body_for("pe"))
            block.scalar(body_for("act"))
            block.vector(body_for("dve"))
            block.gpsimd(body_for("pool"))
            block.sync(body_for("sp"))
        nc.all_engine_barrier()
        nc.clear_and_free_semaphores(list(sems.values()) + dsems + [ccsem])
        nc.all_engine_barrier()
        return dict(n_ops=len(ops), n_waits=nwaits[0], sig=dict(cnt))


class Arena:
    def __init__(self, t, nwords):
        self.t = t
        self.n = nwords
        self.top = 0
        self.peak = 0

    def alloc(self, nelem, dtype=F32):
        words = nelem if dtype == F32 else (nelem + 1) // 2
        off = self.top
        self.top += words
        self.peak = max(self.peak, self.top)
        assert self.top <= self.n, ("SBUF arena overflow", self.top, self.n)
        ap = self.t[:, off:off + words]
        if dtype != F32:
            ap = ap.bitcast(dtype)[:, 0:nelem]
        return ap

    def mark(self):
        return self.top

    def release(self, m):
        self.top = m


class T:
    __slots__ = ("ap", "res")

    def __init__(self, ap, res=None):
        self.ap = ap
        self.res = res if res is not None else Res()


def na_pairs():
    pairs = []
    for eq in range(2, 18):
        es = list(range(eq - 2, eq + 3))
        if eq == 2:
            es.append(5)
        if eq == 17:
            es = [14] + es
        pairs.append((eq, es))
    return pairs


def build_nc(stage=99, debug=()):
    nc = bass.Bass("TRN2", target_bir_lowering=False)
    P = Prog()
    stack = contextlib.ExitStack()
    dbg_out = {}

    def din(name, shape, dt=F32):
        return nc.dram_tensor(name, list(shape), dt, kind="ExternalInput").ap()

    x_ext = din("x_ext", [EXT, D])
    p_c = din("p_c", [TOK, 256])
    x_oth = din("x_oth", [3 * TOK, D])
    x_nbr = din("x_nbr", [128, D])
    w_in = din("w_in", [D, DIN])
    wg = din("wg", [D, 128])
    w_out = din("w_out", [D, D])
    w_ff1 = din("w_ff1", [D, 4 * D])
    w_ff2 = din("w_ff2", [4 * D, D])
    w_pg = din("w_pg", [D, D])
    w_pu = din("w_pu", [256, D])
    gvec = din("gvec", [128, 24])
    convw = din("convw", [128, 8 * 3])
    convb = din("convb", [128, 8])
    gbias = din("gbias", [64, 2])
    mng = din("mng", [128, 512])
    gqk = din("gqk", [128, 8])
    gtab = din("gtab", [128, 8 * 7 * 128])
    rmcol_d = din("rmcol", [128, 2 * 82])
    ccc_d = din("ccc", [128, 2 * 3 * 4])
    ident_d = din("ident", [128, 128])
    selc_d = din("selc", [64, 4 * 128 + 64 + 4])
    maskc_d = din("maskc", [128, 2 * 512])
    out_d = nc.dram_tensor("out", [TOK, D], F32, kind="ExternalOutput").ap()
    ag_in = nc.dram_tensor("ag_in", [128, 1048], F32)
    w1b = nc.dram_tensor("w1b", [D, 4 * D], BF16)
    winb = nc.dram_tensor("winb", [D, DIN], BF16)
    wgb = nc.dram_tensor("wgb", [D, 128], BF16)
    wob = nc.dram_tensor("wob", [D, D], BF16)
    wpgb = nc.dram_tensor("wpgb", [D, D], BF16)
    wpub = nc.dram_tensor("wpub", [256, D], BF16)
    w2b = nc.dram_tensor("w2b", [4 * D, D], BF16)
    ag_out = nc.dram_tensor("ag_out", [3 * 128, 1048], F32)

    AW = 52000
    arena_t = stack.enter_context(nc.sbuf_tensor("arena", [128, AW], F32))
    ar = Arena(arena_t, AW)
    ps_t = stack.enter_context(nc.psum_tensor("ps", [128, 8, 512], F32))

    def bank(b):
        return ps_t[:, b, :]

    def bank_bf(b):
        return ps_t[:, b, :].bitcast(BF16)

    psres = [Res("psum%d" % b) for b in range(8)]

    def mm(out, lhsT, rhs, start, stop, r, w):
        P.add("pe", lambda e: e.matmul(out, lhsT, rhs, start=start, stop=stop), r=r, w=w)

    def tr(out, in_, ident, r, w):
        P.add("pe", lambda e: e.transpose(out, in_, ident), r=r, w=w)

    def act(out, in_, func, r, w, bias=None, scale=None, accum=None, eng="act"):
        kw = {}
        if bias is not None:
            kw["bias"] = bias
        if scale is not None:
            kw["scale"] = scale
        if accum is not None:
            kw["accum_out"] = accum
        P.add("act", lambda e: e.activation(out, in_, func, **kw), r=r, w=w)

    def tt(eng, out, in0, in1, op, r, w):
        P.add(eng, lambda e: e.tensor_tensor(out, in0, in1, op), r=r, w=w)

    def ts(eng, out, in0, s1, s2, op0, op1, r, w):
        if s2 is None:
            P.add(eng, lambda e: e.tensor_scalar(out, in0, s1, None, op0), r=r, w=w)
        else:
            P.add(eng, lambda e: e.tensor_scalar(out, in0, s1, s2, op0, op1), r=r, w=w)

    def stt(eng, out, in0, scalar, in1, op0, op1, r, w):
        P.add(eng, lambda e: e.scalar_tensor_tensor(out, in0, scalar, in1, op0, op1), r=r, w=w)

    def cp(eng, out, in_, r, w):
        if eng == "act":
            P.add("act", lambda e: e.copy(out, in_), r=r, w=w)
        else:
            P.add(eng, lambda e: e.tensor_copy(out, in_), r=r, w=w)

    def recip(out, in_, r, w):
        P.add("dve", lambda e: e.reciprocal(out, in_), r=r, w=w)

    def red(out, in_, op, r, w):
        P.add("dve", lambda e: e.tensor_reduce(out, in_, AX.X, op), r=r, w=w)

    def mset(eng, ap, val, w):
        P.add(eng, lambda e: e.memset(ap, val), r=(), w=w)

    def dma(eng, out, in_, r, w):
        P.add(eng, lambda e: e.dma_start(out=out, in_=in_), r=r, w=w, dma=True)

    def bc(ap, shape):
        return ap.broadcast_to(list(shape))

    def norm_pipeline(n, src_of, gT, dst_of, dres_of, after=None, tbanks=(0, 1)):
        xs3 = [T(ar.alloc(D)) for _ in range(3)]
        xn2 = [T(ar.alloc(D, BF16)) for _ in range(2)]
        jk = T(ar.alloc(D, BF16))
        sm = ar.alloc(16)
        ss3 = [T(sm[:, i:i + 1]) for i in range(3)]
        rt3 = [T(sm[:, 4 + i:5 + i]) for i in range(3)]
        rr3 = [T(sm[:, 8 + i:9 + i]) for i in range(3)]
        for step in range(n + 3):
            i = step
            if i < n:
                dma("sp", xs3[i % 3].ap, src_of(i), r=(), w=[xs3[i % 3].res])
            i = step - 1
            if 0 <= i < n:
                x, q = xs3[i % 3], i % 3
                act(jk.ap, x.ap, AF.Square, r=[x.res], w=[jk.res, ss3[q].res], accum=ss3[q].ap)
                act(rt3[q].ap, ss3[q].ap, AF.Sqrt, r=[ss3[q].res], w=[rt3[q].res], scale=1.0 / D, bias=EPS)
                recip(rr3[q].ap, rt3[q].ap, r=[rt3[q].res], w=[rr3[q].res])
            i = step - 2
            if 0 <= i < n:
                x, q, xn = xs3[i % 3], i % 3, xn2[i % 2]
                act(xn.ap, x.ap, AF.Copy, r=[x.res, rr3[q].res], w=[xn.res], scale=rr3[q].ap)
            i = step - 3
            if 0 <= i < n:
                xn = xn2[i % 2]
                pb = tbanks[i % 2]
                pst = bank_bf(pb).rearrange("p (k t) -> p k t", k=8)
                for k in range(8):
                    tr(pst[:, k, :], xn.ap[:, k * 128:(k + 1) * 128], ident.ap, r=[xn.res, ident.res], w=[psres[pb]])
                tt("dve", dst_of(i), pst, bc(gT.unsqueeze(2), [128, 8, 128]), ALU.mult,
                   r=[psres[pb], gv.res], w=[dres_of(i)])
                if after is not None:
                    after(i)

    def dbg(name, ap, res):
        if False:
            return
        shape = list(ap.shape)
        dt = ap.dtype
        o = nc.dram_tensor("dbg_" + name, shape, dt, kind="ExternalOutput").ap()
        dbg_out[name] = (shape, dt)
        dma("sp", o, ap, r=res, w=())

    ident = T(ar.alloc(128, BF16))
    dma("pool", ident.ap, ident_d, r=(), w=[ident.res])
    gv = T(ar.alloc(24))
    dma("sp", gv.ap, gvec, r=(), w=[gv.res])
    g1T, g2T, gpT = gv.ap[:, 0:8], gv.ap[:, 8:16], gv.ap[:, 16:24]
    gqk_t = T(ar.alloc(8))
    dma("sp", gqk_t.ap, gqk, r=(), w=[gqk_t.res])
    rmcol = T(ar.alloc(2 * 82))
    dma("sp", rmcol.ap, rmcol_d, r=(), w=[rmcol.res])

    yT = ar.alloc(8 * TOK, BF16).rearrange("p (k t) -> p k t", k=8)
    yT_res = [Res("yT%d" % i) for i in range(NT)]
    m_mixer = ar.mark()
    uT = ar.alloc(8 * EXT, BF16).rearrange("p (k t) -> p k t", k=8)
    uT_res = [Res("uT%d" % i) for i in range(NE)]

    qT = ar.alloc(8 * TOK, BF16).rearrange("p (h t) -> p h t", h=8)
    qT_res = [Res() for _ in range(NE)]
    kT = ar.alloc(4 * EXT, BF16).rearrange("p (j t) -> p j t", j=4)
    kT_res = [Res() for _ in range(NE)]
    vext = ar.alloc(NE * 8 * 65, BF16).rearrange("p (e h c) -> p e h c", e=NE, h=8)
    vext_res = [Res() for _ in range(NE)]
    m_na = ar.mark()

    winb_res, wgb_res, wob_res, wpgb_res, wpub_res = Res(), Res(), Res(), Res(), Res()
    winb_na_res = Res()
    for k in range(2):
        dma("pool", winb.ap()[k * 512:(k + 1) * 512, 2064:3600], w_in[k * 512:(k + 1) * 512, 2064:3600], r=(), w=[winb_na_res])
    wn = T(ar.alloc(8 * 1536, BF16))
    wn_v = wn.ap.rearrange("p (k c) -> p k c", k=8)
    dma("sp", wn_v, winb.ap()[:, 2064:3600].rearrange("(k p) c -> p k c", p=128), r=[winb_na_res], w=[wn.res])
    for k in range(4):
        dma("pool", winb.ap()[k * 256:(k + 1) * 256, 0:2064], w_in[k * 256:(k + 1) * 256, 0:2064], r=(), w=[winb_res])
    dma("pool", wgb.ap(), wg, r=(), w=[wgb_res])
    w1b_res, w2b_res = Res(), Res()
    if stage >= 4:
        for k in range(8):
            dma("pool", w1b.ap()[k * 128:(k + 1) * 128, :], w_ff1[k * 128:(k + 1) * 128, :], r=(), w=[w1b_res])
        for k in range(8):
            dma("pool", w2b.ap()[k * 512:(k + 1) * 512, :], w_ff2[k * 512:(k + 1) * 512, :], r=(), w=[w2b_res])
        dma("pool", wob.ap(), w_out, r=(), w=[wob_res])
        dma("pool", wpgb.ap(), w_pg, r=(), w=[wpgb_res])
        dma("pool", wpub.ap(), w_pu, r=(), w=[wpub_res])
    sq = T(ar.alloc(512))
    qn = [T(ar.alloc(512, BF16)) for _ in range(2)]
    small = ar.alloc(64)
    ss = [T(small[:, i:i + 1]) for i in range(2)]
    rt = [T(small[:, 2 + i:3 + i]) for i in range(2)]
    rr = [T(small[:, 4 + i:5 + i]) for i in range(2)]
    ssq = [T(small[:, 8 + 8 * i:16 + 8 * i]) for i in range(2)]
    rq = [T(small[:, 24 + 8 * i:32 + 8 * i]) for i in range(2)]
    rq2 = [T(small[:, 40 + 8 * i:48 + 8 * i]) for i in range(2)]

    P.add("dve", lambda e: e.memset(qT, 0.0), r=(), w=qT_res)
    P.add("dve", lambda e: e.memset(vext[:, :, :, 64:65], 1.0), r=(), w=vext_res)

    def norm_to_T(xs, e, gT, dstT, dst_res, pbank, par):
        act(junk.ap, xs.ap, AF.Square, r=[xs.res], w=[junk.res, ss[par].res], accum=ss[par].ap)
        act(rt[par].ap, ss[par].ap, AF.Sqrt, r=[ss[par].res], w=[rt[par].res], scale=1.0 / D, bias=EPS)
        recip(rr[par].ap, rt[par].ap, r=[rt[par].res], w=[rr[par].res])
        act(xnb[par].ap, xs.ap, AF.Copy, r=[xs.res, rr[par].res], w=[xnb[par].res], scale=rr[par].ap)
        pst = bank_bf(pbank).rearrange("p (k t) -> p k t", k=8)
        for k in range(8):
            tr(pst[:, k, :], xnb[par].ap[:, k * 128:(k + 1) * 128], ident.ap,
               r=[xnb[par].res, ident.res], w=[psres[pbank]])
        tt("dve", dstT[:, :, e * 128:(e + 1) * 128], pst, bc(gT.unsqueeze(2), [128, 8, 128]), ALU.mult,
           r=[psres[pbank], gv.res], w=[dst_res])

    sq4 = [[sq, T(ar.alloc(512))], [T(ar.alloc(512)), T(ar.alloc(512))]]
    qn4 = [[qn[0], qn[1]], [T(ar.alloc(512, BF16)), T(ar.alloc(512, BF16))]]
    sm4 = ar.alloc(64)
    ssq4 = [[T(sm4[:, 8 * (2 * a_ + b_):8 * (2 * a_ + b_) + 8]) for b_ in range(2)] for a_ in range(2)]
    rq4 = [[T(sm4[:, 32 + 8 * (2 * a_ + b_):40 + 8 * (2 * a_ + b_)]) for b_ in range(2)] for a_ in range(2)]
    rqi4 = [[T(ar.alloc(8)) for b_ in range(2)] for a_ in range(2)]

    def na_p1(e):
        tp = e % 2
        own = 2 <= e < 2 + NT
        cgs = [0, 1, 2] if own else [1, 2]
        if stage < 1:
            cgs = []
        for cg in cgs:
            pb = 2 + 3 * tp + cg
            for k in range(8):
                mm(bank(pb), uT[:, k, e * 128:(e + 1) * 128], wn_v[:, k, cg * 512:(cg + 1) * 512],
                   start=(k == 0), stop=(k == 7), r=[uT_res[e], wn.res], w=[psres[pb]])
        for cg in cgs:
            pb = 2 + 3 * tp + cg
            if cg == 2:
                act(vext[:, e, :, 0:64], bank(pb).rearrange("p (h d) -> p h d", h=8), AF.Copy,
                    r=[psres[pb]], w=[vext_res[e]])
                continue
            s_, ss_, r_, ri_, qn_ = sq4[tp][cg], ssq4[tp][cg], rq4[tp][cg], rqi4[tp][cg], qn4[tp][cg]
            psv = bank(pb)
            act(s_.ap, psv, AF.Square, r=[psres[pb]], w=[s_.res])
            red(ss_.ap, s_.ap.rearrange("p (h d) -> p h d", h=8), ALU.add, r=[s_.res], w=[ss_.res])
            if cg == 0:
                act(r_.ap, ss_.ap, AF.Sqrt, r=[ss_.res], w=[r_.res], scale=1.0, bias=64.0 * EPS)
            else:
                act(r_.ap, ss_.ap, AF.Sqrt, r=[ss_.res], w=[r_.res], scale=1.0 / 64, bias=EPS)
            recip(ri_.ap, r_.ap, r=[r_.res], w=[ri_.res])
            tt("dve", qn_.ap.rearrange("p (h d) -> p h d", h=8), psv.rearrange("p (h d) -> p h d", h=8),
               bc(ri_.ap.unsqueeze(2), [128, 8, 64]), ALU.mult, r=[psres[pb], ri_.res], w=[qn_.res])

    def na_p2(e):
        tp = e % 2
        own = 2 <= e < 2 + NT
        if stage < 1:
            return
        tb = 2 + 3 * tp + 2
        for cg in ([0, 1] if own else [1]):
            qn_ = qn4[tp][cg]
            pst = bank_bf(tb)[:, cg * 512:(cg + 1) * 512].rearrange("p (j t) -> p j t", j=4)
            for j in range(4):
                tr(pst[:, j, :], qn_.ap[:, j * 128:(j + 1) * 128], ident.ap, r=[qn_.res, ident.res], w=[psres[tb]])
            if cg == 0:
                t0 = (e - 2) * 128
                for hf in range(2):
                    lo = hf * 64
                    tt("dve", qT[lo:lo + 64, hf::2, t0:t0 + 128], pst[lo:lo + 64, :, :],
                       bc(gqk_t.ap[lo:lo + 64, 0:4].unsqueeze(2), [64, 4, 128]), ALU.mult,
                       r=[psres[tb], gqk_t.res], w=[qT_res[e]])
            else:
                tt("dve", kT[:, :, e * 128:(e + 1) * 128], pst, bc(gqk_t.ap[:, 4:8].unsqueeze(2), [128, 4, 128]),
                   ALU.mult, r=[psres[tb], gqk_t.res], w=[kT_res[e]])

    def na_after(e):
        na_p1(e)
        if e >= 1:
            na_p2(e - 1)

    norm_pipeline(NE, lambda e: x_ext[e * 128:(e + 1) * 128, :], g1T,
                  lambda e: uT[:, :, e * 128:(e + 1) * 128], lambda e: uT_res[e], after=na_after, tbanks=(0, 1))
    na_p2(NE - 1)
    if "uT" in debug:
        dbg("uT", uT, uT_res)
    if "qT" in debug:
        dbg("qT", qT, qT_res)
        dbg("kT", kT, kT_res)
        dbg("vext", vext, vext_res)

    if stage >= 2:
        P.barrier()
        ar.release(m_na)
        EG = T(ar.alloc(8 * 7 * 128, BF16))
        EG_v = EG.ap.rearrange("p (h d c) -> p h d c", h=8, d=7)
        gst = [T(ar.alloc(7 * 128)) for _ in range(2)]
        for h in range(8):
            dma("sp", gst[h % 2].ap, gtab[:, h * 896:(h + 1) * 896], r=(), w=[gst[h % 2].res])
            act(EG_v[:, h, :, :], gst[h % 2].ap.rearrange("p (d c) -> p d c", d=7), AF.Exp,
                r=[gst[h % 2].res], w=[EG.res])
        PT = [T(ar.alloc(1024, BF16)) for _ in range(2)]
        PT2 = [T(ar.alloc(1024, BF16)) for _ in range(6)]
        rden = [T(ar.alloc(8)) for _ in range(2)]
        yb = [T(ar.alloc(512, BF16)) for _ in range(2)]
        pi = 0
        cnt = 0
        for qi, (eq, es) in enumerate(na_pairs()):
            ab = 4 + 2 * (qi % 2)
            accres = [psres[ab], psres[ab + 1]]
            t0 = (eq - 2) * 128
            for ei, e in enumerate(es):
                sb = 2 * (cnt % 2)
                par = cnt % 2
                cnt += 1
                for h in range(8):
                    j = h // 2
                    mm(ps_t[:, sb + h // 4, (h % 4) * 128:(h % 4 + 1) * 128],
                       kT[:, j, e * 128:(e + 1) * 128], qT[:, h, t0:t0 + 128],
                       start=True, stop=True, r=[kT_res[e], qT_res[eq]], w=[psres[sb + h // 4]])
                ptv = PT[par].ap.rearrange("p (h q) -> p h q", h=8)
                pt4 = PT[par].ap.rearrange("p (a h q) -> p a h q", a=2, h=4)
                for b in range(2):
                    src = ps_t[:, sb:sb + 2, :].rearrange("p a (h q) -> p a h q", h=4)[:, :, :, b * 64:(b + 1) * 64]
                    act(pt4[:, :, :, b * 64:(b + 1) * 64], src, AF.Exp,
                        r=[psres[sb], psres[sb + 1], rmcol.res], w=[PT[par].res],
                        bias=rmcol.ap[:, 2 * pi + b:2 * pi + b + 1])
                delta = e - eq
                tt("dve", PT2[ei].ap.rearrange("p (h q) -> p h q", h=8), ptv, EG_v[:, :, delta + 3, :], ALU.mult,
                   r=[PT[par].res, EG.res], w=[PT2[ei].res])
                pi += 1
            for h in range(8):
                for ei, e in enumerate(es):
                    p2v = PT2[ei].ap.rearrange("p (h q) -> p h q", h=8)
                    mm(ps_t[:, ab + h // 4, (h % 4) * 65:(h % 4) * 65 + 65], p2v[:, h, :], vext[:, e, h, :],
                       start=(ei == 0), stop=(ei == len(es) - 1), r=[PT2[ei].res, vext_res[e]],
                       w=[accres[h // 4]])
            qpar = qi % 2
            if 'norm' not in NA_PARTS:
                continue
            accv = ps_t[:, ab:ab + 2, 0:260].rearrange("p a (h c) -> p a h c", c=65)
            recip(rden[qpar].ap.rearrange("p (a h) -> p a h", a=2), accv[:, :, :, 64], r=accres, w=[rden[qpar].res])
            tt("dve", yb[qpar].ap.rearrange("p (a h d) -> p a h d", a=2, h=4), accv[:, :, :, 0:64],
               bc(rden[qpar].ap.rearrange("p (a h) -> p a h", a=2).unsqueeze(3), [128, 2, 4, 64]), ALU.mult,
               r=accres + [rden[qpar].res], w=[yb[qpar].res])
            pst = bank_bf(ab)[:, 0:512].rearrange("p (j t) -> p j t", j=4)
            for j in range(4):
                tr(pst[:, j, :], yb[qpar].ap[:, j * 128:(j + 1) * 128], ident.ap,
                   r=[yb[qpar].res, ident.res], w=[psres[ab]])
            cp("act", yT[:, 4:8, t0:t0 + 128], pst, r=[psres[ab]], w=[yT_res[eq - 2]])
        if "ybT" in debug:
            dbg("ybT", yT[:, 4:8, :], yT_res)

    agout_res = Res()
    NR = 3

    def others_pass():
        DIRS = ((0, 4), (32, 36))

        def rev(ap2d):
            return ap2d[:, ::-1]

        for ob in range(NR):
            P.barrier()
            ar.release(m_mixer)
            kO = ar.alloc(4 * TOK, BF16).rearrange("p (s t) -> p s t", s=4)
            kO_res = [Res() for _ in range(4)]
            GI, GF, WS, BC = [T(ar.alloc(TOK)) for _ in range(4)]
            ZR = WS
            vm = ar.alloc(NT * 4 * 129, BF16).rearrange("p (t h c) -> p t h c", t=NT, h=4)
            vm_res = [Res() for _ in range(NT)]
            ktm = ar.alloc(NT * 512, BF16).rearrange("p (c h d) -> p c h d", c=NT, h=4)
            ktm_res = [Res() for _ in range(NT)]
            selt = T(ar.alloc(580))
            dma("sp", selt.ap[0:64, :], selc_d, r=(), w=[selt.res])
            I4 = selt.ap[0:64, 576:580]
            uTo = ar.alloc(8 * TOK, BF16).rearrange("p (k t) -> p k t", k=8)
            uTo_res = [Res() for _ in range(NT)]
            uTh = T(ar.alloc(8 * 128, BF16))
            uTh_v = uTh.ap.rearrange("p (k t) -> p k t", k=8)
            small = ar.alloc(64)
            ss = [T(small[:, i:i + 1]) for i in range(2)]
            rt = [T(small[:, 2 + i:3 + i]) for i in range(2)]
            rr = [T(small[:, 4 + i:5 + i]) for i in range(2)]
            amax = T(small[:, 8:9])
            namax = T(small[:, 9:10])
            fb = T(small[:, 10:12])
            wf = [T(ar.alloc(8 * 128, BF16))] * 2
            wgt = T(ar.alloc(8 * 128, BF16))
            raw = T(ar.alloc(TOK + 2))
            cacc = T(ar.alloc(TOK))
            wt = T(cacc.ap.bitcast(BF16), cacc.res)
            cw = T(ar.alloc(24))
            cb = T(ar.alloc(8))
            gbt = T(ar.alloc(2))
            fbrep = T(ar.alloc(256))
            wscol = T(ar.alloc(128))
            wscol_v = wscol.ap.rearrange("p (c r h) -> p c r h", c=NT, r=2)
            pay = T(ar.alloc(1048))
            kws = [T(ar.alloc(512, BF16))] * 2
            dma("sp", cw.ap, convw, r=(), w=[cw.res])
            dma("sp", cb.ap, convb, r=(), w=[cb.res])
            dma("sp", gbt.ap[0:64, :], gbias, r=(), w=[gbt.res])
            P.add("pool", lambda e, vm=vm: e.memset(vm[:, :, :, 128:129], 1.0), r=(), w=vm_res)
            for rb in (WS, BC):
                mset("pool", rb.ap[0:64, :], 0.0, w=[rb.res])
            mset("pool", small[:, 8:12], 0.0, w=[amax.res, namax.res, fb.res])

            def src_of(i):
                if i == 0:
                    return x_nbr
                return x_oth[(ob * NT + i - 1) * 128:(ob * NT + i) * 128, :]

            norm_pipeline(NT + 1, src_of, g1T,
                          lambda i: uTh_v if i == 0 else uTo[:, :, (i - 1) * 128:i * 128],
                          lambda i: uTh.res if i == 0 else uTo_res[i - 1], tbanks=(6, 7))
            for fc in range(4, 8):
                w = wf[fc % 2]
                wv = w.ap.rearrange("p (k c) -> p k c", k=8)
                dma("sp", wv, winb.ap()[:, fc * 128:(fc + 1) * 128].rearrange("(k p) c -> p k c", p=128), r=[winb_res], w=[w.res])
                for k in range(8):
                    for g in range(4):
                        mm(bank(g), wv[:, k, :], uTo[:, k, g * 512:(g + 1) * 512], start=(k == 0), stop=(k == 7),
                           r=[w.res] + uTo_res[4 * g:4 * g + 4], w=[psres[g]])
                    mm(bank(4)[:, 0:2], wv[:, k, :], uTh_v[:, k, 2 * ob:2 * ob + 2], start=(k == 0), stop=(k == 7),
                       r=[w.res, uTh.res], w=[psres[4]])
                for g in range(4):
                    cp("act", raw.ap[:, 1 + g * 512:1 + (g + 1) * 512], bank(g), r=[psres[g]], w=[raw.res])
                cp("act", raw.ap[:, 0:TOK + 2:TOK + 1], bank(4)[:, 0:2], r=[psres[4]], w=[raw.res])
                ts("dve", cacc.ap, raw.ap[:, 0:TOK], cw.ap[:, fc * 3:fc * 3 + 1], cb.ap[:, fc:fc + 1], ALU.mult, ALU.add,
                   r=[raw.res, cw.res, cb.res], w=[cacc.res])
                stt("dve", cacc.ap, raw.ap[:, 1:TOK + 1], cw.ap[:, fc * 3 + 1:fc * 3 + 2], cacc.ap, ALU.mult, ALU.add,
                    r=[raw.res, cw.res, cacc.res], w=[cacc.res])
                stt("dve", cacc.ap, raw.ap[:, 2:TOK + 2], cw.ap[:, fc * 3 + 2:fc * 3 + 3], cacc.ap, ALU.mult, ALU.add,
                    r=[raw.res, cw.res, cacc.res], w=[cacc.res])
                act(kO[:, fc - 4, :], cacc.ap, AF.Silu, r=[cacc.res], w=[kO_res[fc - 4]])
            wgt_v = wgt.ap.rearrange("p (k c) -> p k c", k=8)
            dma("sp", wgt_v, wgb.ap().rearrange("(k p) c -> p k c", p=128), r=[wgb_res], w=[wgt.res])
            for g in range(4):
                for k in range(8):
                    mm(bank(5)[0:64, :], wgt_v[:, k, 0:64], uTo[:, k, g * 512:(g + 1) * 512], start=(k == 0), stop=(k == 7),
                       r=[wgt.res] + uTo_res[4 * g:4 * g + 4], w=[psres[5]])
                for k in range(8):
                    mm(bank(6)[0:64, :], wgt_v[:, k, 64:128], uTo[:, k, g * 512:(g + 1) * 512], start=(k == 0), stop=(k == 7),
                       r=[wgt.res] + uTo_res[4 * g:4 * g + 4], w=[psres[6]])
                act(GI.ap[0:64, g * 512:(g + 1) * 512], bank(5)[0:64, :], AF.Identity, r=[psres[5], gbt.res], w=[GI.res],
                    bias=gbt.ap[0:64, 0:1])
                act(GF.ap[0:64, g * 512:(g + 1) * 512], bank(6)[0:64, :], AF.Identity, r=[psres[6], gbt.res], w=[GF.res],
                    bias=gbt.ap[0:64, 1:2])
            wv = wt.ap.rearrange("p (k c) -> p k c", k=8)
            dma("sp", wv, winb.ap()[:, 1024:1536].rearrange("(k p) c -> p k c", p=128), r=[winb_res], w=[wt.res])
            for t in range(NT):
                pb = t % 4
                for k in range(8):
                    mm(bank(pb), uTo[:, k, t * 128:(t + 1) * 128], wv[:, k, :], start=(k == 0), stop=(k == 7),
                       r=[uTo_res[t], wt.res], w=[psres[pb]])
                act(vm[:, t, :, 0:128], bank(pb).rearrange("p (h d) -> p h d", h=4), AF.Copy, r=[psres[pb]], w=[vm_res[t]])
            LF = GF
            act(LF.ap[0:64, :], GF.ap[0:64, :], AF.Exp, r=[GF.res], w=[LF.res], scale=-1.0)
            act(LF.ap[0:64, :], LF.ap[0:64, :], AF.Ln, r=[LF.res], w=[LF.res], bias=1.0)
            ts("dve", LF.ap[0:64, :], LF.ap[0:64, :], -1.0, None, ALU.mult, None, r=[LF.res], w=[LF.res])
            P.add("dve", lambda e, BC=BC, LF=LF, ZR=ZR: e.tensor_tensor_scan(BC.ap[0:4, :], LF.ap[0:4, :], ZR.ap[0:4, :], 0.0,
                                                                          ALU.add, ALU.add), r=[LF.res, ZR.res], w=[BC.res])
            P.add("dve", lambda e, BC=BC, LF=LF, ZR=ZR: e.tensor_tensor_scan(rev(BC.ap[32:36, :]), rev(LF.ap[32:36, :]),
                                                                          rev(ZR.ap[32:36, :]), 0.0, ALU.add, ALU.add),
                  r=[LF.res, ZR.res], w=[BC.res])
            tt("dve", WS.ap[0:64, :], GI.ap[0:64, :], BC.ap[0:64, :], ALU.subtract, r=[GI.res, BC.res], w=[WS.res])
            red(amax.ap[0:64, :], WS.ap[0:64, :], ALU.max, r=[WS.res], w=[amax.res])
            ts("dve", namax.ap[0:64, :], amax.ap[0:64, :], -1.0, None, ALU.mult, None, r=[amax.res], w=[namax.res])
            act(WS.ap[0:64, :], WS.ap[0:64, :], AF.Exp, r=[WS.res, namax.res], w=[WS.res], bias=namax.ap[0:64, :])
            for c in range(NT):
                for dr, (lo, hi) in enumerate(DIRS):
                    col = (c * 2 + dr) * 4
                    mm(bank(0)[:, col:col + 4], WS.ap[lo:hi, c * 128:(c + 1) * 128], I4[lo:hi, :], start=True, stop=True,
                       r=[WS.res, selt.res], w=[psres[0]])
            cp("dve", wscol.ap, bank(0)[:, 0:128], r=[psres[0]], w=[wscol.res])
            for c in range(NT):
                pb = 1 + c % 2
                pst = bank_bf(pb)[:, 0:512].rearrange("p (h d) -> p h d", h=4)
                for h in range(4):
                    tr(pst[:, h, :], kO[:, h, c * 128:(c + 1) * 128], ident.ap, r=[kO_res[h], ident.res], w=[psres[pb]])
                cp("dve" if c % 2 else "act", ktm[:, c], pst, r=[psres[pb]], w=[ktm_res[c]])
            for c in range(NT):
                for dr in range(2):
                    kw = kws[(c * 2 + dr) % 2]
                    kwv = kw.ap.rearrange("p (h d) -> p h d", h=4)
                    tt("dve", kwv, ktm[:, c], bc(wscol_v[:, c, dr, :].unsqueeze(2), [128, 4, 128]), ALU.mult,
                       r=[ktm_res[c], wscol.res], w=[kw.res])
                    for h in range(4):
                        idx = dr * 4 + h
                        mm(ps_t[:, idx, 0:129], kwv[:, h, :], vm[:, c, h, :], start=(c == 0), stop=(c == NT - 1),
                           r=[kw.res, vm_res[c]], w=[psres[idx]])
            for idx in range(8):
                if idx % 2:
                    act(pay.ap[:, idx * 129:(idx + 1) * 129], ps_t[:, idx, 0:129], AF.Copy, r=[psres[idx]], w=[pay.res],
                        scale=float(128 ** -0.5))
                else:
                    ts("dve", pay.ap[:, idx * 129:(idx + 1) * 129], ps_t[:, idx, 0:129], float(128 ** -0.5), None, ALU.mult, None,
                       r=[psres[idx]], w=[pay.res])
            cp("dve", fb.ap[0:4, 0:1], BC.ap[0:4, TOK - 1:TOK], r=[BC.res], w=[fb.res])
            cp("dve", fb.ap[32:36, 0:1], BC.ap[32:36, 0:1], r=[BC.res], w=[fb.res])
            tt("dve", fb.ap[0:64, 1:2], fb.ap[0:64, 0:1], amax.ap[0:64, :], ALU.add, r=[fb.res, amax.res], w=[fb.res])
            cp("dve", fbrep.ap[0:64, :].rearrange("p (q m) -> p q m", q=2), bc(fb.ap[0:64, :].unsqueeze(2), [64, 2, 128]),
               r=[fb.res], w=[fbrep.res])
            for q in range(2):
                for dr, (lo, hi) in enumerate(DIRS):
                    col = q * 8 + dr * 4
                    mm(bank(7)[:, col:col + 4], fbrep.ap[lo:hi, q * 128:(q + 1) * 128], I4[lo:hi, :], start=True, stop=True,
                       r=[fbrep.res, selt.res], w=[psres[7]])
            cp("dve", pay.ap[:, 1032:1048], bank(7)[:, 0:16], r=[psres[7]], w=[pay.res])
            dma("sp", ag_out.ap()[ob * 128:(ob + 1) * 128, :], pay.ap, r=[pay.res], w=[agout_res])

    if stage >= 3:
        others_pass()

    def mlstm_pass():
        P.barrier()
        ar.release(m_mixer)
        qk = ar.alloc(8 * TOK, BF16).rearrange("p (s t) -> p s t", s=8)
        qk_res = [Res() for _ in range(8)]
        ROW = [T(ar.alloc(TOK)) for _ in range(2)]
        GI, GF = ROW[0], ROW[1]
        vm = ar.alloc(NT * 4 * 129, BF16).rearrange("p (t h c) -> p t h c", t=NT, h=4)
        vm_res = [Res() for _ in range(NT)]
        sigo = ar.alloc(NT * 512, BF16).rearrange("p (t c) -> p t c", t=NT)
        sigo_res = [Res() for _ in range(NT)]
        selt = T(ar.alloc(580))
        dma("sp", selt.ap[0:64, :], selc_d, r=(), w=[selt.res])
        sel_v = selt.ap[0:64, 0:512].rearrange("p (h s) -> p h s", h=4)
        diag4 = selt.ap[0:64, 512:576].rearrange("p (c h) -> p c h", c=16)
        I4 = selt.ap[0:64, 576:580]
        m1 = ar.mark()
        uTo = ar.alloc(8 * TOK, BF16).rearrange("p (k t) -> p k t", k=8)
        uTo_res = [Res() for _ in range(NT)]
        uTh = T(ar.alloc(8 * 256, BF16))
        uTh_v = uTh.ap.rearrange("p (k t) -> p k t", k=8)
        small = ar.alloc(64)
        ss = [T(small[:, i:i + 1]) for i in range(2)]
        rt = [T(small[:, 2 + i:3 + i]) for i in range(2)]
        rr = [T(small[:, 4 + i:5 + i]) for i in range(2)]
        wf = [T(ar.alloc(8 * 128, BF16)) for _ in range(2)]
        wgt = T(ar.alloc(8 * 128, BF16))
        raw = [T(ar.alloc(TOK + 2))] * 2
        cacc = T(ar.alloc(TOK))
        wt = [T(cacc.ap.bitcast(BF16), cacc.res)] * 2
        stmp = T(ar.alloc(TOK, BF16))
        cw = T(ar.alloc(24))
        cb = T(ar.alloc(8))
        gbt = T(ar.alloc(2))
        dma("sp", cw.ap, convw, r=(), w=[cw.res])
        dma("sp", cb.ap, convb, r=(), w=[cb.res])
        dma("sp", gbt.ap[0:64, :], gbias, r=(), w=[gbt.res])
        P.add("pool", lambda e: e.memset(vm[:, :, :, 128:129], 1.0), r=(), w=vm_res)

        tiles = [(1, uTh_v[:, :, 0:128], uTh.res), (18, uTh_v[:, :, 128:256], uTh.res)]
        tiles += [(t + 2, uTo[:, :, t * 128:(t + 1) * 128], uTo_res[t]) for t in range(NT)]
        norm_pipeline(len(tiles), lambda i: x_ext[tiles[i][0] * 128:(tiles[i][0] + 1) * 128, :], g1T,
                      lambda i: tiles[i][1], lambda i: tiles[i][2], tbanks=(6, 7))

        for fc in range(8):
            w = wf[fc % 2]
            wv = w.ap.rearrange("p (k c) -> p k c", k=8)
            dma("sp", wv, winb.ap()[:, fc * 128:(fc + 1) * 128].rearrange("(k p) c -> p k c", p=128), r=[winb_res], w=[w.res])
            for k in range(8):
                for g in range(4):
                    mm(bank(g), wv[:, k, :], uTo[:, k, g * 512:(g + 1) * 512], start=(k == 0), stop=(k == 7),
                       r=[w.res] + uTo_res[4 * g:4 * g + 4], w=[psres[g]])
                mm(bank(4)[:, 0:2], wv[:, k, :], uTh_v[:, k, 127:129], start=(k == 0), stop=(k == 7),
                   r=[w.res, uTh.res], w=[psres[4]])
            rw = raw[fc % 2]
            for g in range(4):
                cp("act", rw.ap[:, 1 + g * 512:1 + (g + 1) * 512], bank(g), r=[psres[g]], w=[rw.res])
            cp("act", rw.ap[:, 0:TOK + 2:TOK + 1], bank(4)[:, 0:2], r=[psres[4]], w=[rw.res])
            ts("dve", cacc.ap, rw.ap[:, 0:TOK], cw.ap[:, fc * 3:fc * 3 + 1], cb.ap[:, fc:fc + 1], ALU.mult, ALU.add,
               r=[rw.res, cw.res, cb.res], w=[cacc.res])
            stt("dve", cacc.ap, rw.ap[:, 1:TOK + 1], cw.ap[:, fc * 3 + 1:fc * 3 + 2], cacc.ap, ALU.mult, ALU.add,
                r=[rw.res, cw.res, cacc.res], w=[cacc.res])
            stt("dve", cacc.ap, rw.ap[:, 2:TOK + 2], cw.ap[:, fc * 3 + 2:fc * 3 + 3], cacc.ap, ALU.mult, ALU.add,
                r=[rw.res, cw.res, cacc.res], w=[cacc.res])
            if fc < 4:
                act(qk[:, fc, :], cacc.ap, AF.Silu, r=[cacc.res], w=[qk_res[fc]])
            else:
                act(stmp.ap, cacc.ap, AF.Silu, r=[cacc.res], w=[stmp.res])
                ts("dve", qk[:, fc, :], stmp.ap, float(128 ** -0.5), None, ALU.mult, None, r=[stmp.res], w=[qk_res[fc]])

        wgt_v = wgt.ap.rearrange("p (k c) -> p k c", k=8)
        dma("sp", wgt_v, wgb.ap().rearrange("(k p) c -> p k c", p=128), r=[wgb_res], w=[wgt.res])
        for g in range(4):
            for k in range(8):
                mm(bank(5)[0:64, :], wgt_v[:, k, 0:64], uTo[:, k, g * 512:(g + 1) * 512], start=(k == 0), stop=(k == 7),
                   r=[wgt.res] + uTo_res[4 * g:4 * g + 4], w=[psres[5]])
            for k in range(8):
                mm(bank(6)[0:64, :], wgt_v[:, k, 64:128], uTo[:, k, g * 512:(g + 1) * 512], start=(k == 0), stop=(k == 7),
                   r=[wgt.res] + uTo_res[4 * g:4 * g + 4], w=[psres[6]])
            act(GI.ap[0:64, g * 512:(g + 1) * 512], bank(5)[0:64, :], AF.Identity, r=[psres[5], gbt.res], w=[GI.res],
                bias=gbt.ap[0:64, 0:1])
            act(GF.ap[0:64, g * 512:(g + 1) * 512], bank(6)[0:64, :], AF.Identity, r=[psres[6], gbt.res], w=[GF.res],
                bias=gbt.ap[0:64, 1:2])

        for ci, c0 in enumerate((1024, 1536)):
            w = wt[ci]
            wv = w.ap.rearrange("p (k c) -> p k c", k=8)
            dma("sp", wv, winb.ap()[:, c0:c0 + 512].rearrange("(k p) c -> p k c", p=128), r=[winb_res], w=[w.res])
            for t in range(NT):
                pb = t % 4
                for k in range(8):
                    mm(bank(pb), uTo[:, k, t * 128:(t + 1) * 128], wv[:, k, :], start=(k == 0), stop=(k == 7),
                       r=[uTo_res[t], w.res], w=[psres[pb]])
                if ci == 0:
                    act(vm[:, t, :, 0:128], bank(pb).rearrange("p (h d) -> p h d", h=4), AF.Copy,
                        r=[psres[pb]], w=[vm_res[t]])
                else:
                    act(sigo[:, t, :], bank(pb), AF.Sigmoid, r=[psres[pb]], w=[sigo_res[t]])
        if "qk" in debug:
            dbg("qk", qk, qk_res)
            dbg("GI", GI.ap[0:64, :], [GI.res])
            dbg("GF", GF.ap[0:64, :], [GF.res])

        if MSUB < 2:
            return
        P.barrier()
        ar.release(m1)
        ktm = ar.alloc(NT * 512, BF16).rearrange("p (c h d) -> p c h d", c=NT, h=4)
        ktm_res = [Res() for _ in range(NT)]
        ROW += [T(ar.alloc(TOK)) for _ in range(2)]
        colq = T(ar.alloc(384))
        colq_v = colq.ap.rearrange("p (c r q h) -> p c r q h", c=NT, r=2, q=3)
        wpbc = T(ar.alloc(128))
        wpbc_v = wpbc.ap.rearrange("p (r c h) -> p r c h", r=2, c=NT)
        Cst = [T(ar.alloc(516)) for _ in range(2)]
        Cb = [T(ar.alloc(516, BF16)) for _ in range(2)]
        maskc = T(ar.alloc(1024, BF16))
        dma("pool", maskc.ap, maskc_d, r=(), w=[maskc.res])
        mngt = T(ar.alloc(512))
        dma("sp", mngt.ap, mng, r=(), w=[mngt.res])
        m2 = ar.mark()
        ROW += [T(ar.alloc(TOK)) for _ in range(2)]
        LF, WS, BLC, BC, ZR = ROW[1], ROW[2], ROW[3], ROW[4], ROW[5]
        sm2 = ar.alloc(256)
        amax = T(sm2[:, 0:1])
        namax = T(sm2[:, 1:2])
        fb = T(sm2[:, 2:4])
        minit = T(sm2[:, 4:5])
        mprevc = T(sm2[:, 16:32])
        bst = T(sm2[:, 32:48])
        alast = T(sm2[:, 48:64])
        wpv = T(sm2[:, 64:80])
        Wd = T(sm2[:, 80:144])
        mrow = T(sm2[:, 144:208])
        one1 = T(sm2[:, 208:209])
        fbrep = T(ar.alloc(256))
        ones64 = T(ar.alloc(128))
        wscol = T(ar.alloc(128))
        wscol_v = wscol.ap.rearrange("p (c r h) -> p c r h", c=NT, r=2)
        pay = T(ar.alloc(1048))
        kws = [T(ar.alloc(512, BF16)) for _ in range(2)]
        for rb in (WS, BLC, BC, ZR):
            mset("pool", rb.ap[0:64, :], 0.0, w=[rb.res])
        mset("pool", sm2[:, 0:256], 0.0, w=[amax.res, namax.res, fb.res, minit.res, mprevc.res, bst.res, alast.res,
                                             wpv.res, Wd.res, mrow.res, one1.res])
        mset("dve", one1.ap, 1.0, w=[one1.res])
        mset("pool", ones64.ap[0:64, :], 1.0, w=[ones64.res])
        DIRS = ((0, 4), (32, 36))

        def rev(ap2d):
            return ap2d[:, ::-1]

        act(LF.ap[0:64, :], GF.ap[0:64, :], AF.Exp, r=[GF.res], w=[LF.res], scale=-1.0)
        act(LF.ap[0:64, :], LF.ap[0:64, :], AF.Ln, r=[LF.res], w=[LF.res], bias=1.0)
        ts("dve", LF.ap[0:64, :], LF.ap[0:64, :], -1.0, None, ALU.mult, None, r=[LF.res], w=[LF.res])
        if MSUB < 2.1:
            return
        P.add("dve", lambda e: e.tensor_tensor_scan(BC.ap[0:4, :], LF.ap[0:4, :], ZR.ap[0:4, :], 0.0, ALU.add, ALU.add),
              r=[LF.res, ZR.res], w=[BC.res])
        P.add("dve", lambda e: e.tensor_tensor_scan(rev(BC.ap[32:36, :]), rev(LF.ap[32:36, :]), rev(ZR.ap[32:36, :]), 0.0,
                                                    ALU.add, ALU.add), r=[LF.res, ZR.res], w=[BC.res])
        for c in range(NT):
            pb = 1 + c % 2
            pst = bank_bf(pb)[:, 0:512].rearrange("p (h d) -> p h d", h=4)
            for h in range(4):
                tr(pst[:, h, :], qk[:, 4 + h, c * 128:(c + 1) * 128], ident.ap, r=[qk_res[4 + h], ident.res], w=[psres[pb]])
            cp("dve", ktm[:, c], pst, r=[psres[pb]], w=[ktm_res[c]])
        ago = ag_out.ap().rearrange("(r p) c -> p r c", p=128)

        gsc = T(ar.alloc(NR * 16))
        gsc_v = gsc.ap.rearrange("p (r c) -> p r c", r=NR)
        cct = T(ar.alloc(2 * NR * (NR + 1)))
        cct_v = cct.ap.rearrange("p (d i k) -> p d i k", d=2, i=NR)
        dma("sp", cct.ap, ccc_d, r=(), w=[cct.res])
        dma("sp", gsc_v, ago[:, :, 1032:1048], r=[agout_res], w=[gsc.res])
        if MSUB < 4:
            return
        prod = T(ar.alloc(256))
        Em = [T(ar.alloc(32)) for _ in range(2)]
        coef = [T(ar.alloc(32)) for _ in range(2)]
        minr = [T(ar.alloc(4)) for _ in range(2)]
        min_ = [T(ar.alloc(4)) for _ in range(2)]
        gbuf = [T(ar.alloc(516)) for _ in range(2)]
        for dr in range(2):
            pv4 = prod.ap[:, 0:NR * 4 * NR].rearrange("p (i h k) -> p i h k", i=NR, h=4)
            selk = bc(cct_v[:, dr, :, 0:NR].unsqueeze(2), [128, NR, 4, NR])
            Fv = bc(gsc_v[:, :, dr * 4:dr * 4 + 4].rearrange("p k h -> p h k").unsqueeze(1), [128, NR, 4, NR])
            tt("dve", pv4, selk, Fv, ALU.mult, r=[cct.res, gsc.res], w=[prod.res])
            Ev = Em[dr].ap[:, 0:NR * 4].rearrange("p (i h) -> p i h", i=NR)
            red(Ev, pv4, ALU.add, r=[prod.res], w=[Em[dr].res])
            tt("dve", Ev, Ev, gsc_v[:, :, 8 + dr * 4:12 + dr * 4], ALU.add, r=[Em[dr].res, gsc.res], w=[Em[dr].res])
            tt("dve", Ev, Ev, bc(cct_v[:, dr, :, NR:NR + 1], [128, NR, 4]), ALU.add, r=[Em[dr].res, cct.res], w=[Em[dr].res])
            red(minr[dr].ap, Ev.rearrange("p i h -> p h i"), ALU.max, r=[Em[dr].res], w=[minr[dr].res])
            ts("dve", min_[dr].ap, minr[dr].ap, 0.0, None, ALU.max, None, r=[minr[dr].res], w=[min_[dr].res])
            tt("dve", Ev, Ev, bc(min_[dr].ap.unsqueeze(1), [128, NR, 4]), ALU.subtract, r=[Em[dr].res, min_[dr].res],
               w=[Em[dr].res])
            act(coef[dr].ap[:, 0:NR * 4], Em[dr].ap[:, 0:NR * 4], AF.Exp, r=[Em[dr].res], w=[coef[dr].res])
            cv = coef[dr].ap[:, 0:NR * 4].rearrange("p (i h) -> p i h", i=NR)
            eng = "dve"
            for i in range(NR):
                gb = gbuf[i % 2]
                dma("sp", gb.ap, ag_out.ap()[i * 128:(i + 1) * 128, dr * 516:(dr + 1) * 516], r=[agout_res], w=[gb.res])
                for h in range(4):
                    dst = Cst[dr].ap[:, h * 129:(h + 1) * 129]
                    src = gb.ap[:, h * 129:(h + 1) * 129]
                    if i == 0:
                        ts(eng, dst, src, cv[:, 0, h:h + 1], None, ALU.mult, None, r=[gb.res, coef[dr].res], w=[Cst[dr].res])
                    else:
                        stt(eng, dst, src, cv[:, i, h:h + 1], dst, ALU.mult, ALU.add, r=[gb.res, coef[dr].res, Cst[dr].res],
                            w=[Cst[dr].res])
            cp("act", Cb[dr].ap, Cst[dr].ap, r=[Cst[dr].res], w=[Cb[dr].res])
        cp("dve", mrow.ap[0:1, 0:4], min_[0].ap[0:1, 0:4], r=[min_[0].res], w=[mrow.res])
        cp("dve", mrow.ap[0:1, 32:36], min_[1].ap[0:1, 0:4], r=[min_[1].res], w=[mrow.res])
        mm(bank(0)[0:64, 0:1], mrow.ap[0:1, 0:64], one1.ap[0:1, 0:1], start=True, stop=True, r=[mrow.res, one1.res], w=[psres[0]])
        cp("dve", minit.ap[0:64, :], bank(0)[0:64, 0:1], r=[psres[0]], w=[minit.res])
        MM = ROW[2]
        P.add("dve", lambda e: e.tensor_tensor_scan(MM.ap[0:4, :], LF.ap[0:4, :], GI.ap[0:4, :], minit.ap[0:4, :], ALU.add, ALU.max),
              r=[LF.res, GI.res, minit.res], w=[MM.res])
        P.add("dve", lambda e: e.tensor_tensor_scan(rev(MM.ap[32:36, :]), rev(LF.ap[32:36, :]), rev(GI.ap[32:36, :]),
                                                    minit.ap[32:36, :], ALU.add, ALU.max),
              r=[LF.res, GI.res, minit.res], w=[MM.res])
        act(ZR.ap[0:64, :], MM.ap[0:64, :], AF.Exp, r=[MM.res], w=[ZR.res], scale=-1.0)
        cp("dve", mprevc.ap[0:4, 1:16], MM.ap[0:4, 127:1920:128], r=[MM.res], w=[mprevc.res])
        cp("dve", mprevc.ap[0:4, 0:1], minit.ap[0:4, :], r=[minit.res], w=[mprevc.res])
        cp("dve", mprevc.ap[32:36, 0:15], MM.ap[32:36, 128:2048:128], r=[MM.res], w=[mprevc.res])
        cp("dve", mprevc.ap[32:36, 15:16], minit.ap[32:36, :], r=[minit.res], w=[mprevc.res])
        cp("dve", bst.ap[0:4, 1:16], BC.ap[0:4, 127:1920:128], r=[BC.res], w=[bst.res])
        cp("dve", bst.ap[32:36, 0:15], BC.ap[32:36, 128:2048:128], r=[BC.res], w=[bst.res])
        v3 = lambda t_: t_.ap[0:64, :].rearrange("p (c l) -> p c l", c=NT)
        b3 = lambda t_: bc(t_.ap[0:64, :].unsqueeze(2), [64, NT, 128])
        tt("dve", v3(BLC), v3(BC), b3(bst), ALU.subtract, r=[BC.res, bst.res], w=[BLC.res])
        tt("dve", MM.ap[0:64, :], BLC.ap[0:64, :], MM.ap[0:64, :], ALU.subtract, r=[BLC.res, MM.res], w=[MM.res])
        tt("dve", BLC.ap[0:64, :], GI.ap[0:64, :], BLC.ap[0:64, :], ALU.subtract, r=[GI.res, BLC.res], w=[BLC.res])
        AL, BE = MM, BLC
        cp("dve", alast.ap[0:4, :], AL.ap[0:4, 127:2048:128], r=[AL.res], w=[alast.res])
        cp("dve", alast.ap[32:36, :], AL.ap[32:36, 0:2048:128], r=[AL.res], w=[alast.res])
        WI, WK = ROW[0], ROW[1]
        tt("dve", v3(WI), v3(AL), b3(mprevc), ALU.add, r=[AL.res, mprevc.res, GI.res], w=[WI.res])
        act(WI.ap[0:64, :], WI.ap[0:64, :], AF.Exp, r=[WI.res], w=[WI.res])
        tt("dve", v3(WK), v3(BE), b3(alast), ALU.add, r=[BE.res, alast.res, LF.res], w=[WK.res])
        act(WK.ap[0:64, :], WK.ap[0:64, :], AF.Exp, r=[WK.res], w=[WK.res])
        tt("dve", wpv.ap[0:64, :], mprevc.ap[0:64, :], alast.ap[0:64, :], ALU.add, r=[mprevc.res, alast.res], w=[wpv.res])
        act(wpv.ap[0:64, :], wpv.ap[0:64, :], AF.Exp, r=[wpv.res], w=[wpv.res])
        tt("dve", Wd.ap[0:64, :].rearrange("p (c h) -> p c h", c=NT), bc(wpv.ap[0:64, :].unsqueeze(2), [64, NT, 4]), diag4,
           ALU.mult, r=[wpv.res, selt.res], w=[Wd.res])
        for c in range(NT):
            for dr, (lo, hi) in enumerate(DIRS):
                for q, rb in enumerate((WI, ZR, WK)):
                    col = ((c * 2 + dr) * 3 + q) * 4
                    mm(bank(1)[:, col:col + 4], rb.ap[lo:hi, c * 128:(c + 1) * 128], I4[lo:hi, :], start=True, stop=True,
                       r=[rb.res, selt.res], w=[psres[1]])
        cp("dve", colq.ap, bank(1)[:, 0:384], r=[psres[1]], w=[colq.res])
        for dr, (lo, hi) in enumerate(DIRS):
            mm(bank(2)[:, dr * 64:(dr + 1) * 64], ones64.ap[lo:hi, :], Wd.ap[lo:hi, :], start=True, stop=True,
               r=[ones64.res, Wd.res], w=[psres[2]])
        cp("dve", wpbc.ap, bank(2)[:, 0:128], r=[psres[2]], w=[wpbc.res])

        if MSUB < 5:
            return
        P.barrier()
        ar.release(m2)
        hbuf = ar.alloc(NT * 512, BF16).rearrange("p (c f) -> p c f", c=NT)
        hbuf_res = [Res() for _ in range(NT)]
        DT = [T(ar.alloc(512, BF16)) for _ in range(2)]
        PTm = [T(ar.alloc(512, BF16)) for _ in range(2)]
        kwl = [T(ar.alloc(512, BF16)) for _ in range(2)]
        tmp1 = [T(ar.alloc(512)) for _ in range(2)]
        tmp2 = [T(ar.alloc(512)) for _ in range(2)]
        hd = [T(ar.alloc(512)) for _ in range(2)]
        hs = T(ar.alloc(512))
        sq2 = T(ar.alloc(512))
        y1 = T(ar.alloc(512))
        ya = T(ar.alloc(512, BF16))
        sm3 = ar.alloc(64)
        t4 = [T(sm3[:, 4 * i:4 * i + 4]) for i in range(2)]
        rinv4 = [T(sm3[:, 8 + 4 * i:12 + 4 * i]) for i in range(2)]
        wr4 = [T(sm3[:, 16 + 4 * i:20 + 4 * i]) for i in range(2)]
        ss4 = T(sm3[:, 24:28])
        rt4 = T(sm3[:, 28:32])
        r4 = T(sm3[:, 32:36])
        q4 = lambda ap: ap.rearrange("p (a r) -> p a r", a=2)

        def post(c, hdt):
            tt("dve", hs.ap, hbuf[:, c, :], hdt.ap, ALU.add, r=[hbuf_res[c], hdt.res], w=[hs.res])
            act(sq2.ap, hs.ap, AF.Square, r=[hs.res], w=[sq2.res])
            red(ss4.ap, sq2.ap.rearrange("p (h d) -> p h d", h=4), ALU.add, r=[sq2.res], w=[ss4.res])
            act(rt4.ap, ss4.ap, AF.Sqrt, r=[ss4.res], w=[rt4.res], scale=1.0 / 128, bias=EPS)
            recip(r4.ap, rt4.ap, r=[rt4.res], w=[r4.res])
            tt("dve", y1.ap.rearrange("p (h d) -> p h d", h=4), hs.ap.rearrange("p (h d) -> p h d", h=4),
               bc(r4.ap.unsqueeze(2), [128, 4, 128]), ALU.mult, r=[hs.res, r4.res], w=[y1.res])
            tt("dve", y1.ap, y1.ap, mngt.ap, ALU.mult, r=[y1.res, mngt.res], w=[y1.res])
            tt("dve", ya.ap, y1.ap, sigo[:, c, :], ALU.mult, r=[y1.res, sigo_res[c]], w=[ya.res])
            pst = bank_bf(0)[:, 0:512].rearrange("p (j t) -> p j t", j=4)
            for j in range(4):
                tr(pst[:, j, :], ya.ap[:, j * 128:(j + 1) * 128], ident.ap, r=[ya.res, ident.res], w=[psres[0]])
            cp("act", yT[:, 0:4, c * 128:(c + 1) * 128], pst, r=[psres[0]], w=[yT_res[c]])

        def do_chunk(dr, c, first, last, par):
            lo, hi = DIRS[dr]
            cs = slice(c * 128, (c + 1) * 128)
            for h in range(4):
                mm(bank(0)[:, h * 128:(h + 1) * 128], qk[:, 4 + h, cs], qk[:, h, cs], start=True, stop=True,
                   r=[qk_res[4 + h], qk_res[h]], w=[psres[0]])
            mm(bank(1), ident.ap, maskc.ap[:, dr * 512:(dr + 1) * 512], start=True, stop=False, r=[ident.res, maskc.res],
               w=[psres[1]])
            for h in range(4):
                mm(bank(1)[:, h * 128:(h + 1) * 128], sel_v[lo:hi, h, :], AL.ap[lo:hi, cs], start=False, stop=False,
                   r=[selt.res, AL.res], w=[psres[1]])
                mm(bank(1)[:, h * 128:(h + 1) * 128], BE.ap[lo:hi, cs], sel_v[lo:hi, h, :], start=False, stop=(h == 3),
                   r=[selt.res, BE.res], w=[psres[1]])
            if LPART < 1:
                return
            act(DT[par].ap, bank(1), AF.Exp, r=[psres[1]], w=[DT[par].res])
            tt("dve", PTm[par].ap, bank(0), DT[par].ap, ALU.mult, r=[psres[0], DT[par].res], w=[PTm[par].res])
            for h in ([] if 'intra' in LSKIP else range(4)):
                mm(ps_t[:, 2 + h // 2, (h % 2) * 129:(h % 2) * 129 + 129], PTm[par].ap[:, h * 128:(h + 1) * 128], vm[:, c, h, :],
                   start=True, stop=True, r=[PTm[par].res, vm_res[c]], w=[psres[2 + h // 2]])
            for h in ([] if 'inter' in LSKIP else range(4)):
                mm(ps_t[:, 4 + h // 2, (h % 2) * 129:(h % 2) * 129 + 129], qk[:, h, cs], Cb[dr].ap[:, h * 129:(h + 1) * 129],
                   start=True, stop=True, r=[qk_res[h], Cb[dr].res], w=[psres[4 + h // 2]])
            if LPART < 2:
                return
            Nv = ps_t[:, 2:4, 0:258].rearrange("p a (r c) -> p a r c", c=129)
            Iv = ps_t[:, 4:6, 0:258].rearrange("p a (r c) -> p a r c", c=129)
            nres, ires = [psres[2], psres[3]], [psres[4], psres[5]]
            wint = q4(colq_v[:, c, dr, 0, :])
            eclv = q4(colq_v[:, c, dr, 1, :])
            tt("dve", q4(t4[par].ap), Iv[:, :, :, 128], wint, ALU.mult, r=ires + [colq.res], w=[t4[par].res])
            tt("dve", q4(t4[par].ap), q4(t4[par].ap), Nv[:, :, :, 128], ALU.add, r=nres + [t4[par].res], w=[t4[par].res])
            tt("dve", q4(wr4[par].ap), q4(t4[par].ap), eclv, ALU.max, r=[t4[par].res, colq.res], w=[wr4[par].res])
            stt("dve", t4[par].ap, t4[par].ap, -1.0, wr4[par].ap, ALU.mult, ALU.max, r=[t4[par].res, wr4[par].res], w=[t4[par].res])
            recip(rinv4[par].ap, t4[par].ap, r=[t4[par].res], w=[rinv4[par].res])
            tt("dve", q4(wr4[par].ap), wint, q4(rinv4[par].ap), ALU.mult, r=[colq.res, rinv4[par].res], w=[wr4[par].res])
            f4 = lambda t_: t_.ap.rearrange("p (a r d) -> p a r d", a=2, r=2)
            tt("dve", f4(tmp1[par]), Iv[:, :, :, 0:128], bc(q4(wr4[par].ap).unsqueeze(3), [128, 2, 2, 128]), ALU.mult,
               r=ires + [wr4[par].res], w=[tmp1[par].res])
            tt("dve", f4(tmp2[par]), Nv[:, :, :, 0:128], bc(q4(rinv4[par].ap).unsqueeze(3), [128, 2, 2, 128]), ALU.mult,
               r=nres + [rinv4[par].res], w=[tmp2[par].res])
            if not last:
                kwv = kwl[par].ap.rearrange("p (h d) -> p h d", h=4)
                tt("dve", kwv, ktm[:, c], bc(colq_v[:, c, dr, 2, :].unsqueeze(2), [128, 4, 128]), ALU.mult,
                   r=[ktm_res[c], colq.res], w=[kwl[par].res])
                for h in range(4):
                    mm(ps_t[:, 6 + h // 2, (h % 2) * 129:(h % 2) * 129 + 129], kwv[:, h, :], vm[:, c, h, :],
                       start=True, stop=True, r=[kwl[par].res, vm_res[c]], w=[psres[6 + h // 2]])
                Uv = ps_t[:, 6:8, 0:258].rearrange("p a (r c) -> p a r c", c=129)
                Cv = Cst[dr].ap.rearrange("p (a r c) -> p a r c", a=2, r=2)
                tt("dve", Cv, Cv, bc(q4(wpbc_v[:, dr, c, :]).unsqueeze(3), [128, 2, 2, 129]), ALU.mult,
                   r=[Cst[dr].res, wpbc.res], w=[Cst[dr].res])
                tt("dve", Cv, Cv, Uv, ALU.add, r=[Cst[dr].res, psres[6], psres[7]], w=[Cst[dr].res])
                cp("act", Cb[dr].ap, Cst[dr].ap, r=[Cst[dr].res], w=[Cb[dr].res])
            if LPART < 3:
                return
            if first:
                tt("dve", hbuf[:, c, :], tmp1[par].ap, tmp2[par].ap, ALU.add, r=[tmp1[par].res, tmp2[par].res], w=[hbuf_res[c]])
            else:
                tt("dve", hd[par].ap, tmp1[par].ap, tmp2[par].ap, ALU.add, r=[tmp1[par].res, tmp2[par].res], w=[hd[par].res])
                post(c, hd[par])

        for step in range(int(os.environ.get('LSTEPS', NT))):
            do_chunk(0, step, first=(step < 8), last=(step == NT - 1), par=0)
            do_chunk(1, NT - 1 - step, first=(step < 8), last=(step == NT - 1), par=1)
        if "yaT" in debug:
            dbg("yaT", yT[:, 0:4, :], yT_res)

    if stage >= 3:
        mlstm_pass()

    if stage >= 4:
        P.barrier()
        ar.release(m_mixer)
        wo = T(ar.alloc(8 * D, BF16))
        wo_v = wo.ap.rearrange("p (k c) -> p k c", k=8)
        dma("sp", wo_v, wob.ap().rearrange("(k p) c -> p k c", p=128), r=[wob_res], w=[wo.res])
        wpg = T(ar.alloc(8 * D, BF16))
        wpg_v = wpg.ap.rearrange("p (k c) -> p k c", k=8)
        dma("sp", wpg_v, wpgb.ap().rearrange("(k p) c -> p k c", p=128), r=[wpgb_res], w=[wpg.res])
        wpu = T(ar.alloc(2 * D, BF16))
        wpu_v = wpu.ap.rearrange("p (k c) -> p k c", k=2)
        dma("sp", wpu_v, wpub.ap().rearrange("(k p) c -> p k c", p=128), r=[wpub_res], w=[wpu.res])
        h1 = ar.alloc(4 * D).rearrange("p (t c) -> p t c", t=4)
        h1_res = [Res() for _ in range(4)]
        hnT = T(ar.alloc(8 * 512, BF16))
        hnT_v = hnT.ap.rearrange("p (k t) -> p k t", k=8)
        hnT_res = [Res() for _ in range(4)]
        zT = ar.alloc(32 * 512, BF16).rearrange("p (h t) -> p h t", h=32)
        zT_res = [Res() for _ in range(32)]
        w1c = [T(ar.alloc(8 * 512, BF16)) for _ in range(2)]
        w2c = [T(ar.alloc(4 * 512, BF16)) for _ in range(3)]
        xnb = [T(ar.alloc(D, BF16)) for _ in range(2)]
        junk = T(ar.alloc(D, BF16))
        small = ar.alloc(64)
        ss = [T(small[:, i:i + 1]) for i in range(4)]
        rt = [T(small[:, 4 + i:5 + i]) for i in range(4)]
        rr = [T(small[:, 8 + i:9 + i]) for i in range(4)]
        zsq = [T(ar.alloc(512)) for _ in range(2)]
        gate = [T(ar.alloc(D)) for _ in range(2)]
        hpT = [T(ar.alloc(8 * 128, BF16)) for _ in range(4)]
        pst_t = [T(ar.alloc(256)) for _ in range(2)]
        pbf = [T(ar.alloc(256, BF16)) for _ in range(2)]
        pT = [T(ar.alloc(256, BF16)) for _ in range(2)]
        otile = [T(ar.alloc(D)) for _ in range(2)]

        def norm_pipe_sb(gT, dst_of, dres_of, tbanks):
            for step in range(4 + 2):
                i = step
                if i < 4:
                    act(junk.ap, h1[:, i, :], AF.Square, r=[h1_res[i]], w=[junk.res, ss[i].res], accum=ss[i].ap)
                    act(rt[i].ap, ss[i].ap, AF.Sqrt, r=[ss[i].res], w=[rt[i].res], scale=1.0 / D, bias=EPS)
                    recip(rr[i].ap, rt[i].ap, r=[rt[i].res], w=[rr[i].res])
                i = step - 1
                if 0 <= i < 4:
                    act(xnb[i % 2].ap, h1[:, i, :], AF.Copy, r=[h1_res[i], rr[i].res], w=[xnb[i % 2].res], scale=rr[i].ap)
                i = step - 2
                if 0 <= i < 4:
                    pb = tbanks[i % 2]
                    pst = bank_bf(pb).rearrange("p (k t) -> p k t", k=8)
                    for k in range(8):
                        tr(pst[:, k, :], xnb[i % 2].ap[:, k * 128:(k + 1) * 128], ident.ap,
                           r=[xnb[i % 2].res, ident.res], w=[psres[pb]])
                    tt("dve", dst_of(i), pst, bc(gT.unsqueeze(2), [128, 8, 128]), ALU.mult,
                       r=[psres[pb], gv.res], w=[dres_of(i)])

        for g in range(4):
            for t in range(4):
                tg = g * 4 + t
                dma("sp", h1[:, t, :], x_ext[HALO + tg * 128:HALO + (tg + 1) * 128, :], r=(), w=[h1_res[t]])
            for t in range(4):
                tg = g * 4 + t
                for n in range(2):
                    for k in range(8):
                        mm(bank(2 * t + n), yT[:, k, tg * 128:(tg + 1) * 128], wo_v[:, k, n * 512:(n + 1) * 512],
                           start=(k == 0), stop=(k == 7), r=[yT_res[tg], wo.res], w=[psres[2 * t + n]])
            for t in range(4):
                tt("dve", h1[:, t, :].rearrange("p (n c) -> p n c", n=2), ps_t[:, 2 * t:2 * t + 2, :],
                   h1[:, t, :].rearrange("p (n c) -> p n c", n=2), ALU.add, r=[psres[2 * t], psres[2 * t + 1], h1_res[t]],
                   w=[h1_res[t]])
            norm_pipe_sb(g2T, lambda i: hnT_v[:, :, i * 128:(i + 1) * 128], lambda i: hnT_res[i], tbanks=(0, 1))
            for hcb in range(8):
                w = w1c[hcb % 2]
                wv = w.ap.rearrange("p (k c) -> p k c", k=8)
                dma("sp", wv, w1b.ap()[:, hcb * 512:(hcb + 1) * 512].rearrange("(k p) c -> p k c", p=128), r=[w1b_res], w=[w.res])
                for hl in range(4):
                    hc = hcb * 4 + hl
                    pb = hc % 4
                    for k in range(8):
                        mm(bank(pb), wv[:, k, hl * 128:(hl + 1) * 128], hnT_v[:, k, :], start=(k == 0), stop=(k == 7),
                           r=[w.res] + hnT_res, w=[psres[pb]])
                    zq = zsq[hc % 2]
                    act(zq.ap, bank(pb), AF.Square, r=[psres[pb]], w=[zq.res])
                    stt("dve", zT[:, hc, :], bank(pb), 0.0, zq.ap, ALU.is_gt, ALU.mult, r=[psres[pb], zq.res], w=[zT_res[hc]])
            for n in range(2):
                for hcb in range(8):
                    w = w2c[(n * 8 + hcb) % 3]
                    wv = w.ap.rearrange("p (h c) -> p h c", h=4)
                    dma("sp", wv, w2b.ap()[hcb * 512:(hcb + 1) * 512, n * 512:(n + 1) * 512].rearrange("(h p) c -> p h c", p=128),
                        r=[w2b_res], w=[w.res])
                    for hl in range(4):
                        hc = hcb * 4 + hl
                        for t in range(4):
                            mm(bank(4 + t), zT[:, hc, t * 128:(t + 1) * 128], wv[:, hl, :], start=(hc == 0), stop=(hc == 31),
                               r=[zT_res[hc], w.res], w=[psres[4 + t]])
                for t in range(4):
                    tt("dve", h1[:, t, n * 512:(n + 1) * 512], bank(4 + t), h1[:, t, n * 512:(n + 1) * 512], ALU.add,
                       r=[psres[4 + t], h1_res[t]], w=[h1_res[t]])
            for t in range(4):
                tg = g * 4 + t
                dma("sp", pst_t[t % 2].ap, p_c[tg * 128:(tg + 1) * 128, :], r=(), w=[pst_t[t % 2].res]) if t < 2 else None
            norm_pipe_sb(gpT, lambda i: hpT[i].ap.rearrange("p (k t) -> p k t", k=8), lambda i: hpT[i].res, tbanks=(0, 4))
            for t in range(4):
                tg = g * 4 + t
                q = t % 2
                gb, ub = 4 * q, 4 * q + 2
                hv = hpT[t].ap.rearrange("p (k t) -> p k t", k=8)
                if t >= 2:
                    dma("sp", pst_t[q].ap, p_c[tg * 128:(tg + 1) * 128, :], r=(), w=[pst_t[q].res])
                for n in range(2):
                    for k in range(8):
                        mm(bank(gb + n), hv[:, k, :], wpg_v[:, k, n * 512:(n + 1) * 512], start=(k == 0), stop=(k == 7),
                           r=[hpT[t].res, wpg.res], w=[psres[gb + n]])
                act(gate[q].ap.rearrange("p (n c) -> p n c", n=2), ps_t[:, gb:gb + 2, :], AF.Sigmoid,
                    r=[psres[gb], psres[gb + 1]], w=[gate[q].res])
                cp("act", pbf[q].ap, pst_t[q].ap, r=[pst_t[q].res], w=[pbf[q].res])
                ptr = bank_bf(ub)[:, 0:256].rearrange("p (k t) -> p k t", k=2)
                for k in range(2):
                    tr(ptr[:, k, :], pbf[q].ap[:, k * 128:(k + 1) * 128], ident.ap, r=[pbf[q].res, ident.res], w=[psres[ub]])
                cp("dve", pT[q].ap.rearrange("p (k t) -> p k t", k=2), ptr, r=[psres[ub]], w=[pT[q].res])
                for n in range(2):
                    for k in range(2):
                        mm(bank(ub + n), pT[q].ap[:, k * 128:(k + 1) * 128], wpu_v[:, k, n * 512:(n + 1) * 512],
                           start=(k == 0), stop=(k == 1), r=[pT[q].res, wpu.res], w=[psres[ub + n]])
                ot = otile[q]
                tt("dve", ot.ap.rearrange("p (n c) -> p n c", n=2), ps_t[:, ub:ub + 2, :], gate[q].ap.rearrange("p (n c) -> p n c", n=2),
                   ALU.mult, r=[psres[ub], psres[ub + 1], gate[q].res], w=[ot.res])
                tt("pool", ot.ap, ot.ap, h1[:, t, :], ALU.add, r=[ot.res, h1_res[t]], w=[ot.res])
                dma("sp", out_d[tg * 128:(tg + 1) * 128, :], ot.ap, r=[ot.res], w=())

    if stage < 99:
        P.barrier()
        ar.release(m_mixer)
        z = T(ar.alloc(D))
        mset("dve", z.ap, 0.0, w=[z.res])
        for t in range(NT):
            dma("sp", out_d[t * 128:(t + 1) * 128, :], z.ap, r=[z.res], w=())
    info = P.emit(nc, stack)
    info["arena_peak_words"] = ar.peak
    stack.close()
    return nc, info, dbg_out


def host_consts():
    ident = np.eye(128, dtype=np.float32)
    selc = np.zeros((64, 4 * 128 + 64 + 4), np.float32)
    for p in list(range(4)) + list(range(32, 36)):
        h = p % 32
        selc[p, h * 128:(h + 1) * 128] = 1.0
        for c in range(16):
            selc[p, 512 + c * 4 + h] = 1.0
        selc[p, 576 + h] = 1.0
    s = np.arange(128)[:, None]
    l = np.arange(128)[None, :]
    mf = np.where(s <= l, 0.0, NEG).astype(np.float32)
    mb = np.where(s >= l, 0.0, NEG).astype(np.float32)
    maskc = np.concatenate([np.tile(mf, (1, 4)), np.tile(mb, (1, 4))], axis=1)
    return ident, selc, maskc


def host_gtab(rpb):
    a = np.arange(2)[:, None, None, None, None]
    cp_ = np.arange(64)[None, :, None, None, None]
    dl = np.arange(-3, 4)[None, None, :, None, None]
    b = np.arange(2)[None, None, None, :, None]
    c = np.arange(64)[None, None, None, None, :]
    drow = 2 * dl + a - b
    cs = np.clip(c - 8, 0, 48)
    colok = (cp_ >= cs) & (cp_ < cs + 16)
    ridx = np.broadcast_to(drow + 7, (2, 64, 7, 2, 64))
    cidx = np.broadcast_to(np.clip(cp_ - c + 15, 0, 30), (2, 64, 7, 2, 64))
    ok = np.broadcast_to(colok, (2, 64, 7, 2, 64))
    out = np.empty((2, 64, 8, 7, 2, 64), np.float32)
    for h in range(8):
        out[:, :, h] = np.where(ok, rpb[h][ridx, cidx], np.float32(NEG))
    return np.ascontiguousarray(out.reshape(128, 8 * 7 * 128))


def host_rmcol(j):
    R0 = j * 32
    cols = []
    for eq, es in na_pairs():
        for e in es:
            for b in range(2):
                rq = R0 + 2 * (eq - 2) + b
                rs = min(max(rq - 4, 0), 120)
                col = np.empty((2, 64), np.float32)
                for a in range(2):
                    rk = R0 + 2 * (e - 2) + a
                    ok = (0 <= rk < 128) and (rs <= rk < rs + 8)
                    col[a, :] = 0.0 if ok else NEG
                cols.append(col.reshape(128))
    return np.ascontiguousarray(np.stack(cols, axis=1))


def host_ccc(r):
    j = r % 4
    oth = [q for q in range(4) if q != j]
    out = np.zeros((2, 3, 4), np.float32)
    for i, qi in enumerate(oth):
        out[0, i, 3] = 0.0 if qi < j else NEG
        out[1, i, 3] = 0.0 if qi > j else NEG
        for k, qk_ in enumerate(oth):
            if qi < qk_ < j:
                out[0, i, k] = 1.0
            if j < qk_ < qi:
                out[1, i, k] = 1.0
    return np.ascontiguousarray(np.broadcast_to(out.reshape(1, -1), (128, 24)))


def make_in_maps(inputs):
    f = lambda a: np.ascontiguousarray(np.asarray(a, dtype=np.float32))
    x = f(inputs["x"])
    p = f(inputs["p"])[0]
    w_in = f(inputs["w_in"])[0]
    ident, selc, maskc = host_consts()
    wg = np.zeros((D, 128), np.float32)
    gb = f(inputs["gate_b"])[0]
    gbias = np.zeros((64, 2), np.float32)
    wg[:, 0:4] = w_in[:, 2048 + 0:2048 + 4]
    wg[:, 32:36] = w_in[:, 2048 + 8:2048 + 12]
    wg[:, 64:68] = w_in[:, 2048 + 4:2048 + 8]
    wg[:, 96:100] = w_in[:, 2048 + 12:2048 + 16]
    gbias[0:4, 0] = gb[0:4]
    gbias[32:36, 0] = gb[8:12]
    gbias[0:4, 1] = gb[4:8]
    gbias[32:36, 1] = gb[12:16]
    gvec = np.concatenate([f(inputs[k])[0].reshape(8, 128).T for k in ("norm1_g", "norm2_g", "ple_norm_g")], axis=1)
    convw = f(inputs["conv_w"])[0].reshape(3, 8, 128).transpose(2, 1, 0).reshape(128, 24)
    convb = f(inputs["conv_b"])[0].reshape(8, 128).T
    mng = np.broadcast_to(f(inputs["mlstm_norm_g"])[0].reshape(1, 512), (128, 512))
    gq = f(inputs["q_norm_g"])[0].reshape(4, 128).T
    gk = f(inputs["k_norm_g"])[0].reshape(4, 128).T
    gqk = np.concatenate([gq, gk], axis=1)
    gtab = host_gtab(f(inputs["rpb"])[0])
    shared = dict(
        w_in=w_in, wg=wg, w_out=f(inputs["w_out"])[0], w_ff1=f(inputs["w_ff1"])[0], w_ff2=f(inputs["w_ff2"])[0],
        w_pg=f(inputs["w_ple_gate"])[0], w_pu=f(inputs["w_ple_up"])[0], gvec=np.ascontiguousarray(gvec),
        convw=np.ascontiguousarray(convw), convb=np.ascontiguousarray(convb), gbias=gbias,
        mng=np.ascontiguousarray(mng), gqk=np.ascontiguousarray(gqk), gtab=gtab, ident=ident, selc=selc,
        maskc=np.ascontiguousarray(maskc))
    in_maps = []
    for r in range(NCORES):
        b, j = r // 4, r % 4
        t0 = j * TOK
        xe = np.zeros((EXT, D), np.float32)
        lo, hi = t0 - HALO, t0 + TOK + HALO
        slo, shi = max(lo, 0), min(hi, 4 * TOK)
        xe[slo - lo:shi - lo] = x[b, slo:shi]
        m = dict(shared)
        m["x_ext"] = xe
        m["p_c"] = np.ascontiguousarray(p[b, t0:t0 + TOK])
        oth = [q for q in range(4) if q != j]
        m["x_oth"] = np.ascontiguousarray(np.concatenate([x[b, q * TOK:(q + 1) * TOK] for q in oth], axis=0))
        xn = np.zeros((128, D), np.float32)
        for i, q in enumerate(oth):
            if q * TOK - 1 >= 0:
                xn[2 * i] = x[b, q * TOK - 1]
            if (q + 1) * TOK < 4 * TOK:
                xn[2 * i + 1] = x[b, (q + 1) * TOK]
        m["x_nbr"] = xn
        m["rmcol"] = host_rmcol(j)
        m["ccc"] = host_ccc(r)
        in_maps.append(m)
    return in_maps


_NC_CACHE = {}


def kernel(**inputs):
    in_maps = make_in_maps(inputs)
    if "nc" not in _NC_CACHE:
        _NC_CACHE["nc"] = build_nc()[0]
    nc = _NC_CACHE["nc"]
    res = run_bass_kernel_spmd(nc, in_maps, core_ids=list(range(NCORES)))
    out = np.empty((2, 4 * TOK, D), np.float32)
    for r in range(NCORES):
        b, j = r // 4, r % 4
        out[b, j * TOK:(j + 1) * TOK] = res.results[r]["out"]
    return out
```

```python
import contextlib
import os
NA_PARTS = os.environ.get('NA_PARTS', 's,exp,mul,pv,norm').split(',')
XCHG = os.environ.get('XCHG', '1') == '1'
MSUB = float(os.environ.get('MSUB', '9'))
LPART = int(os.environ.get('LPART', '9'))
LSKIP = os.environ.get('LSKIP', '').split(',')
import numpy as np
import concourse.bass as bass
import concourse.mybir as mybir
from concourse.bass_utils import run_bass_kernel_spmd

F32 = mybir.dt.float32
BF16 = mybir.dt.bfloat16
ALU = mybir.AluOpType
AF = mybir.ActivationFunctionType
AX = mybir.AxisListType

NCORES = 8
D = 1024
TOK = 2048
NT = 16
HALO = 256
EXT = TOK + 2 * HALO
NE = EXT // 128
DIN = 3600
NEG = -30000.0
EPS = 1e-6

ENGS = ("pe", "act", "dve", "pool", "sp")
NDSEM = 20
NDSEM_HW = 12


class Res:
    __slots__ = ("name", "lw", "rd")

    def __init__(self, name=""):
        self.name = name
        self.lw = None
        self.rd = []


class Op:
    __slots__ = ("eng", "fn", "deps", "dma", "sig", "sem", "target", "inc", "cc")


class Prog:
    def __init__(self):
        self.ops = []
        self.lastc = {e: None for e in ENGS}
        self.bar = {e: [] for e in ENGS}
        self.dma_since = []

    def add(self, eng, fn, r=(), w=(), dma=False, cc=False):
        idx = len(self.ops)
        deps = set()
        raw = set()
        for x in r:
            if x.lw is not None:
                deps.add(x.lw)
                raw.add(x.lw)
        for x in w:
            if x.lw is not None:
                deps.add(x.lw)
            deps.update(x.rd)
        if not dma and not cc:
            deps = {d for d in deps if d in raw or self.ops[d].dma or self.ops[d].eng != eng}
        if self.bar[eng]:
            deps.update(self.bar[eng])
            self.bar[eng] = []
        op = Op()
        op.eng, op.fn, op.deps, op.dma = eng, fn, deps, dma
        op.sig, op.sem, op.target, op.inc = False, None, 0, 1
        op.cc = cc
        if cc:
            op.dma = True
        self.ops.append(op)
        for x in r:
            x.rd.append(idx)
        for x in w:
            x.lw = idx
            x.rd = []
        if dma or cc:
            self.dma_since.append(idx)
        else:
            self.lastc[eng] = idx
        return idx

    def barrier(self):
        deps = [v for v in self.lastc.values() if v is not None] + list(self.dma_since)
        for e in ENGS:
            self.bar[e] = list(set(self.bar[e]) | set(deps))
        self.dma_since = []

    def emit(self, nc, stack):
        ops = self.ops
        sems = {e: nc.alloc_semaphore(name="s_" + e) for e in ENGS}
        dsems = [nc.alloc_semaphore(name="s_d%d" % i) for i in range(NDSEM)]
        duse = [0] * NDSEM
        dlast = [None] * NDSEM
        k = 0
        kp = 0
        ccsem = nc.alloc_semaphore(name="s_cc")
        for i, op in enumerate(ops):
            if op.cc:
                op.sem, op.target, op.inc, op.sig = ccsem, 1, None, True
                continue
            if op.dma:
                if op.eng == "pool":
                    s = NDSEM_HW + kp % (NDSEM - NDSEM_HW)
                    kp += 1
                else:
                    s = k % NDSEM_HW
                    k += 1
                if dlast[s] is not None:
                    op.deps.add(dlast[s])
                dlast[s] = i
                duse[s] += 1
                op.sem, op.target, op.inc, op.sig = dsems[s], 16 * duse[s], 16, True
        for i, op in enumerate(ops):
            for d in op.deps:
                dop = ops[d]
                if dop.dma:
                    continue
                if dop.eng == "pe" and op.eng == "pe" and not op.dma:
                    continue
                dop.sig = True
        cnt = {e: 0 for e in ENGS}
        for op in ops:
            if op.sig and not op.dma:
                cnt[op.eng] += 1
                op.sem, op.target, op.inc = sems[op.eng], cnt[op.eng], 1
        streams = {e: [op for op in ops if op.eng == e] for e in ENGS}
        semkey = {id(s): s for s in list(sems.values()) + dsems + [ccsem]}
        nwaits = [0]

        def body_for(ename):
            def body(e):
                known = {}
                for op in streams[ename]:
                    waits = {}
                    for d in op.deps:
                        dop = ops[d]
                        if (not dop.dma) and dop.eng == "pe" and ename == "pe" and not op.dma:
                            continue
                        key = id(dop.sem)
                        if waits.get(key, 0) < dop.target:
                            waits[key] = dop.target
                    for key, t in waits.items():
                        if known.get(key, 0) >= t:
                            continue
                        e.wait_ge(semkey[key], t)
                        nwaits[0] += 1
                        known[key] = t
                    ins = op.fn(e)
                    if op.sig:
                        if op.inc is None:
                            ins.then_inc(op.sem)
                        else:
                            ins.then_inc(op.sem, op.inc)
                if ename == "sp":
                    for s in range(NDSEM):
                        if duse[s] and known.get(id(dsems[s]), 0) < 16 * duse[s]:
                            e.wait_ge(dsems[s], 16 * duse[s])
            return body

        with nc.Block() as block:
            block.tensor(body_for("pe"))
            block.scalar(body_for("act"))
            block.vector(body_for("dve"))
            block.gpsimd(body_for("pool"))
            block.sync(body_for("sp"))
        nc.all_engine_barrier()
        nc.clear_and_free_semaphores(list(sems.values()) + dsems + [ccsem])
        nc.all_engine_barrier()
        return dict(n_ops=len(ops), n_waits=nwaits[0], sig=dict(cnt))


class Arena:
    def __init__(self, t, nwords):
        self.t = t
        self.n = nwords
        self.top = 0
        self.peak = 0

    def alloc(self, nelem, dtype=F32):
        words = nelem if dtype == F32 else (nelem + 1) // 2
        off = self.top
        self.top += words
        self.peak = max(self.peak, self.top)
        assert self.top <= self.n, ("SBUF arena overflow", self.top, self.n)
        ap = self.t[:, off:off + words]
        if dtype != F32:
            ap = ap.bitcast(dtype)[:, 0:nelem]
        return ap

    def mark(self):
        return self.top

    def release(self, m):
        self.top = m


class T:
    __slots__ = ("ap", "res")

    def __init__(self, ap, res=None):
        self.ap = ap
        self.res = res if res is not None else Res()


def na_pairs():
    pairs = []
    for eq in range(2, 18):
        es = list(range(eq - 2, eq + 3))
        if eq == 2:
            es.append(5)
        if eq == 17:
            es = [14] + es
        pairs.append((eq, es))
    return pairs


def build_nc(stage=99, debug=()):
    nc = bass.Bass("TRN2", target_bir_lowering=False)
    P = Prog()
    stack = contextlib.ExitStack()
    dbg_out = {}

    def din(name, shape, dt=F32):
        return nc.dram_tensor(name, list(shape), dt, kind="ExternalInput").ap()

    x_ext = din("x_ext", [EXT, D])
    p_c = din("p_c", [TOK, 256])
    x_oth = din("x_oth", [3 * TOK, D])
    x_nbr = din("x_nbr", [128, D])
    w_in = din("w_in", [D, DIN])
    wg = din("wg", [D, 128])
    w_out = din("w_out", [D, D])
    w_ff1 = din("w_ff1", [D, 4 * D])
    w_ff2 = din("w_ff2", [4 * D, D])
    w_pg = din("w_pg", [D, D])
    w_pu = din("w_pu", [256, D])
    gvec = din("gvec", [128, 24])
    convw = din("convw", [128, 8 * 3])
    convb = din("convb", [128, 8])
    gbias = din("gbias", [64, 2])
    mng = din("mng", [128, 512])
    gqk = din("gqk", [128, 8])
    gtab = din("gtab", [128, 8 * 7 * 128])
    rmcol_d = din("rmcol", [128, 2 * 82])
    ccc_d = din("ccc", [128, 2 * 3 * 4])
    ident_d = din("ident", [128, 128])
    selc_d = din("selc", [64, 4 * 128 + 64 + 4])
    maskc_d = din("maskc", [128, 2 * 512])
    out_d = nc.dram_tensor("out", [TOK, D], F32, kind="ExternalOutput").ap()
    ag_in = nc.dram_tensor("ag_in", [128, 1048], F32)
    w1b = nc.dram_tensor("w1b", [D, 4 * D], BF16)
    winb = nc.dram_tensor("winb", [D, DIN], BF16)
    wgb = nc.dram_tensor("wgb", [D, 128], BF16)
    wob = nc.dram_tensor("wob", [D, D], BF16)
    wpgb = nc.dram_tensor("wpgb", [D, D], BF16)
    wpub = nc.dram_tensor("wpub", [256, D], BF16)
    w2b = nc.dram_tensor("w2b", [4 * D, D], BF16)
    ag_out = nc.dram_tensor("ag_out", [3 * 128, 1048], F32)

    AW = 52000
    arena_t = stack.enter_context(nc.sbuf_tensor("arena", [128, AW], F32))
    ar = Arena(arena_t, AW)
    ps_t = stack.enter_context(nc.psum_tensor("ps", [128, 8, 512], F32))

    def bank(b):
        return ps_t[:, b, :]

    def bank_bf(b):
        return ps_t[:, b, :].bitcast(BF16)

    psres = [Res("psum%d" % b) for b in range(8)]

    def mm(out, lhsT, rhs, start, stop, r, w):
        P.add("pe", lambda e: e.matmul(out, lhsT, rhs, start=start, stop=stop), r=r, w=w)

    def tr(out, in_, ident, r, w):
        P.add("pe", lambda e: e.transpose(out, in_, ident), r=r, w=w)

    def act(out, in_, func, r, w, bias=None, scale=None, accum=None, eng="act"):
        kw = {}
        if bias is not None:
            kw["bias"] = bias
        if scale is not None:
            kw["scale"] = scale
        if accum is not None:
            kw["accum_out"] = accum
        P.add("act", lambda e: e.activation(out, in_, func, **kw), r=r, w=w)

    def tt(eng, out, in0, in1, op, r, w):
        P.add(eng, lambda e: e.tensor_tensor(out, in0, in1, op), r=r, w=w)

    def ts(eng, out, in0, s1, s2, op0, op1, r, w):
        if s2 is None:
            P.add(eng, lambda e: e.tensor_scalar(out, in0, s1, None, op0), r=r, w=w)
        else:
            P.add(eng, lambda e: e.tensor_scalar(out, in0, s1, s2, op0, op1), r=r, w=w)

    def stt(eng, out, in0, scalar, in1, op0, op1, r, w):
        P.add(eng, lambda e: e.scalar_tensor_tensor(out, in0, scalar, in1, op0, op1), r=r, w=w)

    def cp(eng, out, in_, r, w):
        if eng == "act":
            P.add("act", lambda e: e.copy(out, in_), r=r, w=w)
        else:
            P.add(eng, lambda e: e.tensor_copy(out, in_), r=r, w=w)

    def recip(out, in_, r, w):
        P.add("dve", lambda e: e.reciprocal(out, in_), r=r, w=w)

    def red(out, in_, op, r, w):
        P.add("dve", lambda e: e.tensor_reduce(out, in_, AX.X, op), r=r, w=w)

    def mset(eng, ap, val, w):
        P.add(eng, lambda e: e.memset(ap, val), r=(), w=w)

    def dma(eng, out, in_, r, w):
        P.add(eng, lambda e: e.dma_start(out=out, in_=in_), r=r, w=w, dma=True)

    def bc(ap, shape):
        return ap.broadcast_to(list(shape))

    def norm_pipeline(n, src_of, gT, dst_of, dres_of, after=None, tbanks=(0, 1)):
        xs3 = [T(ar.alloc(D)) for _ in range(3)]
        xn2 = [T(ar.alloc(D, BF16)) for _ in range(2)]
        jk = T(ar.alloc(D, BF16))
        sm = ar.alloc(16)
        ss3 = [T(sm[:, i:i + 1]) for i in range(3)]
        rt3 = [T(sm[:, 4 + i:5 + i]) for i in range(3)]
        rr3 = [T(sm[:, 8 + i:9 + i]) for i in range(3)]
        for step in range(n + 3):
            i = step
            if i < n:
                dma("sp", xs3[i % 3].ap, src_of(i), r=(), w=[xs3[i % 3].res])
            i = step - 1
            if 0 <= i < n:
                x, q = xs3[i % 3], i % 3
                act(jk.ap, x.ap, AF.Square, r=[x.res], w=[jk.res, ss3[q].res], accum=ss3[q].ap)
                act(rt3[q].ap, ss3[q].ap, AF.Sqrt, r=[ss3[q].res], w=[rt3[q].res], scale=1.0 / D, bias=EPS)
                recip(rr3[q].ap, rt3[q].ap, r=[rt3[q].res], w=[rr3[q].res])
            i = step - 2
            if 0 <= i < n:
                x, q, xn = xs3[i % 3], i % 3, xn2[i % 2]
                act(xn.ap, x.ap, AF.Copy, r=[x.res, rr3[q].res], w=[xn.res], scale=rr3[q].ap)
            i = step - 3
            if 0 <= i < n:
                xn = xn2[i % 2]
                pb = tbanks[i % 2]
                pst = bank_bf(pb).rearrange("p (k t) -> p k t", k=8)
                for k in range(8):
                    tr(pst[:, k, :], xn.ap[:, k * 128:(k + 1) * 128], ident.ap, r=[xn.res, ident.res], w=[psres[pb]])
                tt("dve", dst_of(i), pst, bc(gT.unsqueeze(2), [128, 8, 128]), ALU.mult,
                   r=[psres[pb], gv.res], w=[dres_of(i)])
                if after is not None:
                    after(i)

    def dbg(name, ap, res):
        if False:
            return
        shape = list(ap.shape)
        dt = ap.dtype
        o = nc.dram_tensor("dbg_" + name, shape, dt, kind="ExternalOutput").ap()
        dbg_out[name] = (shape, dt)
        dma("sp", o, ap, r=res, w=())

    ident = T(ar.alloc(128, BF16))
    dma("pool", ident.ap, ident_d, r=(), w=[ident.res])
    gv = T(ar.alloc(24))
    dma("sp", gv.ap, gvec, r=(), w=[gv.res])
    g1T, g2T, gpT = gv.ap[:, 0:8], gv.ap[:, 8:16], gv.ap[:, 16:24]
    gqk_t = T(ar.alloc(8))
    dma("sp", gqk_t.ap, gqk, r=(), w=[gqk_t.res])
    rmcol = T(ar.alloc(2 * 82))
    dma("sp", rmcol.ap, rmcol_d, r=(), w=[rmcol.res])

    yT = ar.alloc(8 * TOK, BF16).rearrange("p (k t) -> p k t", k=8)
    yT_res = [Res("yT%d" % i) for i in range(NT)]
    m_mixer = ar.mark()
    uT = ar.alloc(8 * EXT, BF16).rearrange("p (k t) -> p k t", k=8)
    uT_res = [Res("uT%d" % i) for i in range(NE)]

    qT = ar.alloc(8 * TOK, BF16).rearrange("p (h t) -> p h t", h=8)
    qT_res = [Res() for _ in range(NE)]
    kT = ar.alloc(4 * EXT, BF16).rearrange("p (j t) -> p j t", j=4)
    kT_res = [Res() for _ in range(NE)]
    vext = ar.alloc(NE * 8 * 65, BF16).rearrange("p (e h c) -> p e h c", e=NE, h=8)
    vext_res = [Res() for _ in range(NE)]
    m_na = ar.mark()

    winb_res, wgb_res, wob_res, wpgb_res, wpub_res = Res(), Res(), Res(), Res(), Res()
    winb_na_res = Res()
    for k in range(2):
        dma("pool", winb.ap()[k * 512:(k + 1) * 512, 2064:3600], w_in[k * 512:(k + 1) * 512, 2064:3600], r=(), w=[winb_na_res])
    wn = T(ar.alloc(8 * 1536, BF16))
    wn_v = wn.ap.rearrange("p (k c) -> p k c", k=8)
    dma("sp", wn_v, winb.ap()[:, 2064:3600].rearrange("(k p) c -> p k c", p=128), r=[winb_na_res], w=[wn.res])
    for k in range(4):
        dma("pool", winb.ap()[k * 256:(k + 1) * 256, 0:2064], w_in[k * 256:(k + 1) * 256, 0:2064], r=(), w=[winb_res])
    dma("pool", wgb.ap(), wg, r=(), w=[wgb_res])
    w1b_res, w2b_res = Res(), Res()
    if stage >= 4:
        for k in range(8):
            dma("pool", w1b.ap()[k * 128:(k + 1) * 128, :], w_ff1[k * 128:(k + 1) * 128, :], r=(), w=[w1b_res])
        for k in range(8):
            dma("pool", w2b.ap()[k * 512:(k + 1) * 512, :], w_ff2[k * 512:(k + 1) * 512, :], r=(), w=[w2b_res])
        dma("pool", wob.ap(), w_out, r=(), w=[wob_res])
        dma("pool", wpgb.ap(), w_pg, r=(), w=[wpgb_res])
        dma("pool", wpub.ap(), w_pu, r=(), w=[wpub_res])
    sq = T(ar.alloc(512))
    qn = [T(ar.alloc(512, BF16)) for _ in range(2)]
    small = ar.alloc(64)
    ss = [T(small[:, i:i + 1]) for i in range(2)]
    rt = [T(small[:, 2 + i:3 + i]) for i in range(2)]
    rr = [T(small[:, 4 + i:5 + i]) for i in range(2)]
    ssq = [T(small[:, 8 + 8 * i:16 + 8 * i]) for i in range(2)]
    rq = [T(small[:, 24 + 8 * i:32 + 8 * i]) for i in range(2)]
    rq2 = [T(small[:, 40 + 8 * i:48 + 8 * i]) for i in range(2)]

    P.add("dve", lambda e: e.memset(qT, 0.0), r=(), w=qT_res)
    P.add("dve", lambda e: e.memset(vext[:, :, :, 64:65], 1.0), r=(), w=vext_res)

    def norm_to_T(xs, e, gT, dstT, dst_res, pbank, par):
        act(junk.ap, xs.ap, AF.Square, r=[xs.res], w=[junk.res, ss[par].res], accum=ss[par].ap)
        act(rt[par].ap, ss[par].ap, AF.Sqrt, r=[ss[par].res], w=[rt[par].res], scale=1.0 / D, bias=EPS)
        recip(rr[par].ap, rt[par].ap, r=[rt[par].res], w=[rr[par].res])
        act(xnb[par].ap, xs.ap, AF.Copy, r=[xs.res, rr[par].res], w=[xnb[par].res], scale=rr[par].ap)
        pst = bank_bf(pbank).rearrange("p (k t) -> p k t", k=8)
        for k in range(8):
            tr(pst[:, k, :], xnb[par].ap[:, k * 128:(k + 1) * 128], ident.ap,
               r=[xnb[par].res, ident.res], w=[psres[pbank]])
        tt("dve", dstT[:, :, e * 128:(e + 1) * 128], pst, bc(gT.unsqueeze(2), [128, 8, 128]), ALU.mult,
           r=[psres[pbank], gv.res], w=[dst_res])

    sq4 = [[sq, T(ar.alloc(512))], [T(ar.alloc(512)), T(ar.alloc(512))]]
    qn4 = [[qn[0], qn[1]], [T(ar.alloc(512, BF16)), T(ar.alloc(512, BF16))]]
    sm4 = ar.alloc(64)
    ssq4 = [[T(sm4[:, 8 * (2 * a_ + b_):8 * (2 * a_ + b_) + 8]) for b_ in range(2)] for a_ in range(2)]
    rq4 = [[T(sm4[:, 32 + 8 * (2 * a_ + b_):40 + 8 * (2 * a_ + b_)]) for b_ in range(2)] for a_ in range(2)]
    rqi4 = [[T(ar.alloc(8)) for b_ in range(2)] for a_ in range(2)]

    def na_p1(e):
        tp = e % 2
        own = 2 <= e < 2 + NT
        cgs = [0, 1, 2] if own else [1, 2]
        if stage < 1:
            cgs = []
        for cg in cgs:
            pb = 2 + 3 * tp + cg
            for k in range(8):
                mm(bank(pb), uT[:, k, e * 128:(e + 1) * 128], wn_v[:, k, cg * 512:(cg + 1) * 512],
                   start=(k == 0), stop=(k == 7), r=[uT_res[e], wn.res], w=[psres[pb]])
        for cg in cgs:
            pb = 2 + 3 * tp + cg
            if cg == 2:
                act(vext[:, e, :, 0:64], bank(pb).rearrange("p (h d) -> p h d", h=8), AF.Copy,
                    r=[psres[pb]], w=[vext_res[e]])
                continue
            s_, ss_, r_, ri_, qn_ = sq4[tp][cg], ssq4[tp][cg], rq4[tp][cg], rqi4[tp][cg], qn4[tp][cg]
            psv = bank(pb)
            act(s_.ap, psv, AF.Square, r=[psres[pb]], w=[s_.res])
            red(ss_.ap, s_.ap.rearrange("p (h d) -> p h d", h=8), ALU.add, r=[s_.res], w=[ss_.res])
            if cg == 0:
                act(r_.ap, ss_.ap, AF.Sqrt, r=[ss_.res], w=[r_.res], scale=1.0, bias=64.0 * EPS)
            else:
                act(r_.ap, ss_.ap, AF.Sqrt, r=[ss_.res], w=[r_.res], scale=1.0 / 64, bias=EPS)
            recip(ri_.ap, r_.ap, r=[r_.res], w=[ri_.res])
            tt("dve", qn_.ap.rearrange("p (h d) -> p h d", h=8), psv.rearrange("p (h d) -> p h d", h=8),
               bc(ri_.ap.unsqueeze(2), [128, 8, 64]), ALU.mult, r=[psres[pb], ri_.res], w=[qn_.res])

    def na_p2(e):
        tp = e % 2
        own = 2 <= e < 2 + NT
        if stage < 1:
            return
        tb = 2 + 3 * tp + 2
        for cg in ([0, 1] if own else [1]):
            qn_ = qn4[tp][cg]
            pst = bank_bf(tb)[:, cg * 512:(cg + 1) * 512].rearrange("p (j t) -> p j t", j=4)
            for j in range(4):
                tr(pst[:, j, :], qn_.ap[:, j * 128:(j + 1) * 128], ident.ap, r=[qn_.res, ident.res], w=[psres[tb]])
            if cg == 0:
                t0 = (e - 2) * 128
                for hf in range(2):
                    lo = hf * 64
                    tt("dve", qT[lo:lo + 64, hf::2, t0:t0 + 128], pst[lo:lo + 64, :, :],
                       bc(gqk_t.ap[lo:lo + 64, 0:4].unsqueeze(2), [64, 4, 128]), ALU.mult,
                       r=[psres[tb], gqk_t.res], w=[qT_res[e]])
            else:
                tt("dve", kT[:, :, e * 128:(e + 1) * 128], pst, bc(gqk_t.ap[:, 4:8].unsqueeze(2), [128, 4, 128]),
                   ALU.mult, r=[psres[tb], gqk_t.res], w=[kT_res[e]])

    def na_after(e):
        na_p1(e)
        if e >= 1:
            na_p2(e - 1)

    norm_pipeline(NE, lambda e: x_ext[e * 128:(e + 1) * 128, :], g1T,
                  lambda e: uT[:, :, e * 128:(e + 1) * 128], lambda e: uT_res[e], after=na_after, tbanks=(0, 1))
    na_p2(NE - 1)
    if "uT" in debug:
        dbg("uT", uT, uT_res)
    if "qT" in debug:
        dbg("qT", qT, qT_res)
        dbg("kT", kT, kT_res)
        dbg("vext", vext, vext_res)

    if stage >= 2:
        P.barrier()
        ar.release(m_na)
        EG = T(ar.alloc(8 * 7 * 128, BF16))
        EG_v = EG.ap.rearrange("p (h d c) -> p h d c", h=8, d=7)
        gst = [T(ar.alloc(7 * 128)) for _ in range(2)]
        for h in range(8):
            dma("sp", gst[h % 2].ap, gtab[:, h * 896:(h + 1) * 896], r=(), w=[gst[h % 2].res])
            act(EG_v[:, h, :, :], gst[h % 2].ap.rearrange("p (d c) -> p d c", d=7), AF.Exp,
                r=[gst[h % 2].res], w=[EG.res])
        PT = [T(ar.alloc(1024, BF16)) for _ in range(2)]
        PT2 = [T(ar.alloc(1024, BF16)) for _ in range(6)]
        rden = [T(ar.alloc(8)) for _ in range(2)]
        yb = [T(ar.alloc(512, BF16)) for _ in range(2)]
        pi = 0
        cnt = 0
        for qi, (eq, es) in enumerate(na_pairs()):
            ab = 4 + 2 * (qi % 2)
            accres = [psres[ab], psres[ab + 1]]
            t0 = (eq - 2) * 128
            for ei, e in enumerate(es):
                sb = 2 * (cnt % 2)
                par = cnt % 2
                cnt += 1
                for h in range(8):
                    j = h // 2
                    mm(ps_t[:, sb + h // 4, (h % 4) * 128:(h % 4 + 1) * 128],
                       kT[:, j, e * 128:(e + 1) * 128], qT[:, h, t0:t0 + 128],
                       start=True, stop=True, r=[kT_res[e], qT_res[eq]], w=[psres[sb + h // 4]])
                ptv = PT[par].ap.rearrange("p (h q) -> p h q", h=8)
                pt4 = PT[par].ap.rearrange("p (a h q) -> p a h q", a=2, h=4)
                for b in range(2):
                    src = ps_t[:, sb:sb + 2, :].rearrange("p a (h q) -> p a h q", h=4)[:, :, :, b * 64:(b + 1) * 64]
                    act(pt4[:, :, :, b * 64:(b + 1) * 64], src, AF.Exp,
                        r=[psres[sb], psres[sb + 1], rmcol.res], w=[PT[par].res],
                        bias=rmcol.ap[:, 2 * pi + b:2 * pi + b + 1])
                delta = e - eq
                tt("dve", PT2[ei].ap.rearrange("p (h q) -> p h q", h=8), ptv, EG_v[:, :, delta + 3, :], ALU.mult,
                   r=[PT[par].res, EG.res], w=[PT2[ei].res])
                pi += 1
            for h in range(8):
                for ei, e in enumerate(es):
                    p2v = PT2[ei].ap.rearrange("p (h q) -> p h q", h=8)
                    mm(ps_t[:, ab + h // 4, (h % 4) * 65:(h % 4) * 65 + 65], p2v[:, h, :], vext[:, e, h, :],
                       start=(ei == 0), stop=(ei == len(es) - 1), r=[PT2[ei].res, vext_res[e]],
                       w=[accres[h // 4]])
            qpar = qi % 2
            if 'norm' not in NA_PARTS:
                continue
            accv = ps_t[:, ab:ab + 2, 0:260].rearrange("p a (h c) -> p a h c", c=65)
            recip(rden[qpar].ap.rearrange("p (a h) -> p a h", a=2), accv[:, :, :, 64], r=accres, w=[rden[qpar].res])
            tt("dve", yb[qpar].ap.rearrange("p (a h d) -> p a h d", a=2, h=4), accv[:, :, :, 0:64],
               bc(rden[qpar].ap.rearrange("p (a h) -> p a h", a=2).unsqueeze(3), [128, 2, 4, 64]), ALU.mult,
               r=accres + [rden[qpar].res], w=[yb[qpar].res])
            pst = bank_bf(ab)[:, 0:512].rearrange("p (j t) -> p j t", j=4)
            for j in range(4):
                tr(pst[:, j, :], yb[qpar].ap[:, j * 128:(j + 1) * 128], ident.ap,
                   r=[yb[qpar].res, ident.res], w=[psres[ab]])
            cp("act", yT[:, 4:8, t0:t0 + 128], pst, r=[psres[ab]], w=[yT_res[eq - 2]])
        if "ybT" in debug:
            dbg("ybT", yT[:, 4:8, :], yT_res)

    agout_res = Res()
    NR = 3

    def others_pass():
        DIRS = ((0, 4), (32, 36))

        def rev(ap2d):
            return ap2d[:, ::-1]

        for ob in range(NR):
            P.barrier()
            ar.release(m_mixer)
            kO = ar.alloc(4 * TOK, BF16).rearrange("p (s t) -> p s t", s=4)
            kO_res = [Res() for _ in range(4)]
            GI, GF, WS, BC = [T(ar.alloc(TOK)) for _ in range(4)]
            ZR = WS
            vm = ar.alloc(NT * 4 * 129, BF16).rearrange("p (t h c) -> p t h c", t=NT, h=4)
            vm_res = [Res() for _ in range(NT)]
            ktm = ar.alloc(NT * 512, BF16).rearrange("p (c h d) -> p c h d", c=NT, h=4)
            ktm_res = [Res() for _ in range(NT)]
            selt = T(ar.alloc(580))
            dma("sp", selt.ap[0:64, :], selc_d, r=(), w=[selt.res])
            I4 = selt.ap[0:64, 576:580]
            uTo = ar.alloc(8 * TOK, BF16).rearrange("p (k t) -> p k t", k=8)
            uTo_res = [Res() for _ in range(NT)]
            uTh = T(ar.alloc(8 * 128, BF16))
            uTh_v = uTh.ap.rearrange("p (k t) -> p k t", k=8)
            small = ar.alloc(64)
            ss = [T(small[:, i:i + 1]) for i in range(2)]
            rt = [T(small[:, 2 + i:3 + i]) for i in range(2)]
            rr = [T(small[:, 4 + i:5 + i]) for i in range(2)]
            amax = T(small[:, 8:9])
            namax = T(small[:, 9:10])
            fb = T(small[:, 10:12])
            wf = [T(ar.alloc(8 * 128, BF16))] * 2
            wgt = T(ar.alloc(8 * 128, BF16))
            raw = T(ar.alloc(TOK + 2))
            cacc = T(ar.alloc(TOK))
            wt = T(cacc.ap.bitcast(BF16), cacc.res)
            cw = T(ar.alloc(24))
            cb = T(ar.alloc(8))
            gbt = T(ar.alloc(2))
            fbrep = T(ar.alloc(256))
            wscol = T(ar.alloc(128))
            wscol_v = wscol.ap.rearrange("p (c r h) -> p c r h", c=NT, r=2)
            pay = T(ar.alloc(1048))
            kws = [T(ar.alloc(512, BF16))] * 2
            dma("sp", cw.ap, convw, r=(), w=[cw.res])
            dma("sp", cb.ap, convb, r=(), w=[cb.res])
            dma("sp", gbt.ap[0:64, :], gbias, r=(), w=[gbt.res])
            P.add("pool", lambda e, vm=vm: e.memset(vm[:, :, :, 128:129], 1.0), r=(), w=vm_res)
            for rb in (WS, BC):
                mset("pool", rb.ap[0:64, :], 0.0, w=[rb.res])
            mset("pool", small[:, 8:12], 0.0, w=[amax.res, namax.res, fb.res])

            def src_of(i):
                if i == 0:
                    return x_nbr
                return x_oth[(ob * NT + i - 1) * 128:(ob * NT + i) * 128, :]

            norm_pipeline(NT + 1, src_of, g1T,
                          lambda i: uTh_v if i == 0 else uTo[:, :, (i - 1) * 128:i * 128],
                          lambda i: uTh.res if i == 0 else uTo_res[i - 1], tbanks=(6, 7))
            for fc in range(4, 8):
                w = wf[fc % 2]
                wv = w.ap.rearrange("p (k c) -> p k c", k=8)
                dma("sp", wv, winb.ap()[:, fc * 128:(fc + 1) * 128].rearrange("(k p) c -> p k c", p=128), r=[winb_res], w=[w.res])
                for k in range(8):
                    for g in range(4):
                        mm(bank(g), wv[:, k, :], uTo[:, k, g * 512:(g + 1) * 512], start=(k == 0), stop=(k == 7),
                           r=[w.res] + uTo_res[4 * g:4 * g + 4], w=[psres[g]])
                    mm(bank(4)[:, 0:2], wv[:, k, :], uTh_v[:, k, 2 * ob:2 * ob + 2], start=(k == 0), stop=(k == 7),
                       r=[w.res, uTh.res], w=[psres[4]])
                for g in range(4):
                    cp("act", raw.ap[:, 1 + g * 512:1 + (g + 1) * 512], bank(g), r=[psres[g]], w=[raw.res])
                cp("act", raw.ap[:, 0:TOK + 2:TOK + 1], bank(4)[:, 0:2], r=[psres[4]], w=[raw.res])
                ts("dve", cacc.ap, raw.ap[:, 0:TOK], cw.ap[:, fc * 3:fc * 3 + 1], cb.ap[:, fc:fc + 1], ALU.mult, ALU.add,
                   r=[raw.res, cw.res, cb.res], w=[cacc.res])
                stt("dve", cacc.ap, raw.ap[:, 1:TOK + 1], cw.ap[:, fc * 3 + 1:fc * 3 + 2], cacc.ap, ALU.mult, ALU.add,
                    r=[raw.res, cw.res, cacc.res], w=[cacc.res])
                stt("dve", cacc.ap, raw.ap[:, 2:TOK + 2], cw.ap[:, fc * 3 + 2:fc * 3 + 3], cacc.ap, ALU.mult, ALU.add,
                    r=[raw.res, cw.res, cacc.res], w=[cacc.res])
                act(kO[:, fc - 4, :], cacc.ap, AF.Silu, r=[cacc.res], w=[kO_res[fc - 4]])
            wgt_v = wgt.ap.rearrange("p (k c) -> p k c", k=8)
            dma("sp", wgt_v, wgb.ap().rearrange("(k p) c -> p k c", p=128), r=[wgb_res], w=[wgt.res])
            for g in range(4):
                for k in range(8):
                    mm(bank(5)[0:64, :], wgt_v[:, k, 0:64], uTo[:, k, g * 512:(g + 1) * 512], start=(k == 0), stop=(k == 7),
                       r=[wgt.res] + uTo_res[4 * g:4 * g + 4], w=[psres[5]])
                for k in range(8):
                    mm(bank(6)[0:64, :], wgt_v[:, k, 64:128], uTo[:, k, g * 512:(g + 1) * 512], start=(k == 0), stop=(k == 7),
                       r=[wgt.res] + uTo_res[4 * g:4 * g + 4], w=[psres[6]])
                act(GI.ap[0:64, g * 512:(g + 1) * 512], bank(5)[0:64, :], AF.Identity, r=[psres[5], gbt.res], w=[GI.res],
                    bias=gbt.ap[0:64, 0:1])
                act(GF.ap[0:64, g * 512:(g + 1) * 512], bank(6)[0:64, :], AF.Identity, r=[psres[6], gbt.res], w=[GF.res],
                    bias=gbt.ap[0:64, 1:2])
            wv = wt.ap.rearrange("p (k c) -> p k c", k=8)
            dma("sp", wv, winb.ap()[:, 1024:1536].rearrange("(k p) c -> p k c", p=128), r=[winb_res], w=[wt.res])
            for t in range(NT):
                pb = t % 4
                for k in range(8):
                    mm(bank(pb), uTo[:, k, t * 128:(t + 1) * 128], wv[:, k, :], start=(k == 0), stop=(k == 7),
                       r=[uTo_res[t], wt.res], w=[psres[pb]])
                act(vm[:, t, :, 0:128], bank(pb).rearrange("p (h d) -> p h d", h=4), AF.Copy, r=[psres[pb]], w=[vm_res[t]])
            LF = GF
            act(LF.ap[0:64, :], GF.ap[0:64, :], AF.Exp, r=[GF.res], w=[LF.res], scale=-1.0)
            act(LF.ap[0:64, :], LF.ap[0:64, :], AF.Ln, r=[LF.res], w=[LF.res], bias=1.0)
            ts("dve", LF.ap[0:64, :], LF.ap[0:64, :], -1.0, None, ALU.mult, None, r=[LF.res], w=[LF.res])
            P.add("dve", lambda e, BC=BC, LF=LF, ZR=ZR: e.tensor_tensor_scan(BC.ap[0:4, :], LF.ap[0:4, :], ZR.ap[0:4, :], 0.0,
                                                                          ALU.add, ALU.add), r=[LF.res, ZR.res], w=[BC.res])
            P.add("dve", lambda e, BC=BC, LF=LF, ZR=ZR: e.tensor_tensor_scan(rev(BC.ap[32:36, :]), rev(LF.ap[32:36, :]),
                                                                          rev(ZR.ap[32:36, :]), 0.0, ALU.add, ALU.add),
                  r=[LF.res, ZR.res], w=[BC.res])
            tt("dve", WS.ap[0:64, :], GI.ap[0:64, :], BC.ap[0:64, :], ALU.subtract, r=[GI.res, BC.res], w=[WS.res])
            red(amax.ap[0:64, :], WS.ap[0:64, :], ALU.max, r=[WS.res], w=[amax.res])
            ts("dve", namax.ap[0:64, :], amax.ap[0:64, :], -1.0, None, ALU.mult, None, r=[amax.res], w=[namax.res])
            act(WS.ap[0:64, :], WS.ap[0:64, :], AF.Exp, r=[WS.res, namax.res], w=[WS.res], bias=namax.ap[0:64, :])
            for c in range(NT):
                for dr, (lo, hi) in enumerate(DIRS):
                    col = (c * 2 + dr) * 4
                    mm(bank(0)[:, col:col + 4], WS.ap[lo:hi, c * 128:(c + 1) * 128], I4[lo:hi, :], start=True, stop=True,
                       r=[WS.res, selt.res], w=[psres[0]])
            cp("dve", wscol.ap, bank(0)[:, 0:128], r=[psres[0]], w=[wscol.res])
            for c in range(NT):
                pb = 1 + c % 2
                pst = bank_bf(pb)[:, 0:512].rearrange("p (h d) -> p h d", h=4)
                for h in range(4):
                    tr(pst[:, h, :], kO[:, h, c * 128:(c + 1) * 128], ident.ap, r=[kO_res[h], ident.res], w=[psres[pb]])
                cp("dve" if c % 2 else "act", ktm[:, c], pst, r=[psres[pb]], w=[ktm_res[c]])
            for c in range(NT):
                for dr in range(2):
                    kw = kws[(c * 2 + dr) % 2]
                    kwv = kw.ap.rearrange("p (h d) -> p h d", h=4)
                    tt("dve", kwv, ktm[:, c], bc(wscol_v[:, c, dr, :].unsqueeze(2), [128, 4, 128]), ALU.mult,
                       r=[ktm_res[c], wscol.res], w=[kw.res])
                    for h in range(4):
                        idx = dr * 4 + h
                        mm(ps_t[:, idx, 0:129], kwv[:, h, :], vm[:, c, h, :], start=(c == 0), stop=(c == NT - 1),
                           r=[kw.res, vm_res[c]], w=[psres[idx]])
            for idx in range(8):
                if idx % 2:
                    act(pay.ap[:, idx * 129:(idx + 1) * 129], ps_t[:, idx, 0:129], AF.Copy, r=[psres[idx]], w=[pay.res],
                        scale=float(128 ** -0.5))
                else:
                    ts("dve", pay.ap[:, idx * 129:(idx + 1) * 129], ps_t[:, idx, 0:129], float(128 ** -0.5), None, ALU.mult, None,
                       r=[psres[idx]], w=[pay.res])
            cp("dve", fb.ap[0:4, 0:1], BC.ap[0:4, TOK - 1:TOK], r=[BC.res], w=[fb.res])
            cp("dve", fb.ap[32:36, 0:1], BC.ap[32:36, 0:1], r=[BC.res], w=[fb.res])
            tt("dve", fb.ap[0:64, 1:2], fb.ap[0:64, 0:1], amax.ap[0:64, :], ALU.add, r=[fb.res, amax.res], w=[fb.res])
            cp("dve", fbrep.ap[0:64, :].rearrange("p (q m) -> p q m", q=2), bc(fb.ap[0:64, :].unsqueeze(2), [64, 2, 128]),
               r=[fb.res], w=[fbrep.res])
            for q in range(2):
                for dr, (lo, hi) in enumerate(DIRS):
                    col = q * 8 + dr * 4
                    mm(bank(7)[:, col:col + 4], fbrep.ap[lo:hi, q * 128:(q + 1) * 128], I4[lo:hi, :], start=True, stop=True,
                       r=[fbrep.res, selt.res], w=[psres[7]])
            cp("dve", pay.ap[:, 1032:1048], bank(7)[:, 0:16], r=[psres[7]], w=[pay.res])
            dma("sp", ag_out.ap()[ob * 128:(ob + 1) * 128, :], pay.ap, r=[pay.res], w=[agout_res])

    if stage >= 3:
        others_pass()

    def mlstm_pass():
        P.barrier()
        ar.release(m_mixer)
        qk = ar.alloc(8 * TOK, BF16).rearrange("p (s t) -> p s t", s=8)
        qk_res = [Res() for _ in range(8)]
        ROW = [T(ar.alloc(TOK)) for _ in range(2)]
        GI, GF = ROW[0], ROW[1]
        vm = ar.alloc(NT * 4 * 129, BF16).rearrange("p (t h c) -> p t h c", t=NT, h=4)
        vm_res = [Res() for _ in range(NT)]
        sigo = ar.alloc(NT * 512, BF16).rearrange("p (t c) -> p t c", t=NT)
        sigo_res = [Res() for _ in range(NT)]
        selt = T(ar.alloc(580))
        dma("sp", selt.ap[0:64, :], selc_d, r=(), w=[selt.res])
        sel_v = selt.ap[0:64, 0:512].rearrange("p (h s) -> p h s", h=4)
        diag4 = selt.ap[0:64, 512:576].rearrange("p (c h) -> p c h", c=16)
        I4 = selt.ap[0:64, 576:580]
        m1 = ar.mark()
        uTo = ar.alloc(8 * TOK, BF16).rearrange("p (k t) -> p k t", k=8)
        uTo_res = [Res() for _ in range(NT)]
        uTh = T(ar.alloc(8 * 256, BF16))
        uTh_v = uTh.ap.rearrange("p (k t) -> p k t", k=8)
        small = ar.alloc(64)
        ss = [T(small[:, i:i + 1]) for i in range(2)]
        rt = [T(small[:, 2 + i:3 + i]) for i in range(2)]
        rr = [T(small[:, 4 + i:5 + i]) for i in range(2)]
        wf = [T(ar.alloc(8 * 128, BF16)) for _ in range(2)]
        wgt = T(ar.alloc(8 * 128, BF16))
        raw = [T(ar.alloc(TOK + 2))] * 2
        cacc = T(ar.alloc(TOK))
        wt = [T(cacc.ap.bitcast(BF16), cacc.res)] * 2
        stmp = T(ar.alloc(TOK, BF16))
        cw = T(ar.alloc(24))
        cb = T(ar.alloc(8))
        gbt = T(ar.alloc(2))
        dma("sp", cw.ap, convw, r=(), w=[cw.res])
        dma("sp", cb.ap, convb, r=(), w=[cb.res])
        dma("sp", gbt.ap[0:64, :], gbias, r=(), w=[gbt.res])
        P.add("pool", lambda e: e.memset(vm[:, :, :, 128:129], 1.0), r=(), w=vm_res)

        tiles = [(1, uTh_v[:, :, 0:128], uTh.res), (18, uTh_v[:, :, 128:256], uTh.res)]
        tiles += [(t + 2, uTo[:, :, t * 128:(t + 1) * 128], uTo_res[t]) for t in range(NT)]
        norm_pipeline(len(tiles), lambda i: x_ext[tiles[i][0] * 128:(tiles[i][0] + 1) * 128, :], g1T,
                      lambda i: tiles[i][1], lambda i: tiles[i][2], tbanks=(6, 7))

        for fc in range(8):
            w = wf[fc % 2]
            wv = w.ap.rearrange("p (k c) -> p k c", k=8)
            dma("sp", wv, winb.ap()[:, fc * 128:(fc + 1) * 128].rearrange("(k p) c -> p k c", p=128), r=[winb_res], w=[w.res])
            for k in range(8):
                for g in range(4):
                    mm(bank(g), wv[:, k, :], uTo[:, k, g * 512:(g + 1) * 512], start=(k == 0), stop=(k == 7),
                       r=[w.res] + uTo_res[4 * g:4 * g + 4], w=[psres[g]])
                mm(bank(4)[:, 0:2], wv[:, k, :], uTh_v[:, k, 127:129], start=(k == 0), stop=(k == 7),
                   r=[w.res, uTh.res], w=[psres[4]])
            rw = raw[fc % 2]
            for g in range(4):
                cp("act", rw.ap[:, 1 + g * 512:1 + (g + 1) * 512], bank(g), r=[psres[g]], w=[rw.res])
            cp("act", rw.ap[:, 0:TOK + 2:TOK + 1], bank(4)[:, 0:2], r=[psres[4]], w=[rw.res])
            ts("dve", cacc.ap, rw.ap[:, 0:TOK], cw.ap[:, fc * 3:fc * 3 + 1], cb.ap[:, fc:fc + 1], ALU.mult, ALU.add,
               r=[rw.res, cw.res, cb.res], w=[cacc.res])
            stt("dve", cacc.ap, rw.ap[:, 1:TOK + 1], cw.ap[:, fc * 3 + 1:fc * 3 + 2], cacc.ap, ALU.mult, ALU.add,
                r=[rw.res, cw.res, cacc.res], w=[cacc.res])
            stt("dve", cacc.ap, rw.ap[:, 2:TOK + 2], cw.ap[:, fc * 3 + 2:fc * 3 + 3], cacc.ap, ALU.mult, ALU.add,
                r=[rw.res, cw.res, cacc.res], w=[cacc.res])
            if fc < 4:
                act(qk[:, fc, :], cacc.ap, AF.Silu, r=[cacc.res], w=[qk_res[fc]])
            else:
                act(stmp.ap, cacc.ap, AF.Silu, r=[cacc.res], w=[stmp.res])
                ts("dve", qk[:, fc, :], stmp.ap, float(128 ** -0.5), None, ALU.mult, None, r=[stmp.res], w=[qk_res[fc]])

        wgt_v = wgt.ap.rearrange("p (k c) -> p k c", k=8)
        dma("sp", wgt_v, wgb.ap().rearrange("(k p) c -> p k c", p=128), r=[wgb_res], w=[wgt.res])
        for g in range(4):
            for k in range(8):
                mm(bank(5)[0:64, :], wgt_v[:, k, 0:64], uTo[:, k, g * 512:(g + 1) * 512], start=(k == 0), stop=(k == 7),
                   r=[wgt.res] + uTo_res[4 * g:4 * g + 4], w=[psres[5]])
            for k in range(8):
                mm(bank(6)[0:64, :], wgt_v[:, k, 64:128], uTo[:, k, g * 512:(g + 1) * 512], start=(k == 0), stop=(k == 7),
                   r=[wgt.res] + uTo_res[4 * g:4 * g + 4], w=[psres[6]])
            act(GI.ap[0:64, g * 512:(g + 1) * 512], bank(5)[0:64, :], AF.Identity, r=[psres[5], gbt.res], w=[GI.res],
                bias=gbt.ap[0:64, 0:1])
            act(GF.ap[0:64, g * 512:(g + 1) * 512], bank(6)[0:64, :], AF.Identity, r=[psres[6], gbt.res], w=[GF.res],
                bias=gbt.ap[0:64, 1:2])

        for ci, c0 in enumerate((1024, 1536)):
            w = wt[ci]
            wv = w.ap.rearrange("p (k c) -> p k c", k=8)
            dma("sp", wv, winb.ap()[:, c0:c0 + 512].rearrange("(k p) c -> p k c", p=128), r=[winb_res], w=[w.res])
            for t in range(NT):
                pb = t % 4
                for k in range(8):
                    mm(bank(pb), uTo[:, k, t * 128:(t + 1) * 128], wv[:, k, :], start=(k == 0), stop=(k == 7),
                       r=[uTo_res[t], w.res], w=[psres[pb]])
                if ci == 0:
                    act(vm[:, t, :, 0:128], bank(pb).rearrange("p (h d) -> p h d", h=4), AF.Copy,
                        r=[psres[pb]], w=[vm_res[t]])
                else:
                    act(sigo[:, t, :], bank(pb), AF.Sigmoid, r=[psres[pb]], w=[sigo_res[t]])
        if "qk" in debug:
            dbg("qk", qk, qk_res)
            dbg("GI", GI.ap[0:64, :], [GI.res])
            dbg("GF", GF.ap[0:64, :], [GF.res])

        if MSUB < 2:
            return
        P.barrier()
        ar.release(m1)
        ktm = ar.alloc(NT * 512, BF16).rearrange("p (c h d) -> p c h d", c=NT, h=4)
        ktm_res = [Res() for _ in range(NT)]
        ROW += [T(ar.alloc(TOK)) for _ in range(2)]
        colq = T(ar.alloc(384))
        colq_v = colq.ap.rearrange("p (c r q h) -> p c r q h", c=NT, r=2, q=3)
        wpbc = T(ar.alloc(128))
        wpbc_v = wpbc.ap.rearrange("p (r c h) -> p r c h", r=2, c=NT)
        Cst = [T(ar.alloc(516)) for _ in range(2)]
        Cb = [T(ar.alloc(516, BF16)) for _ in range(2)]
        maskc = T(ar.alloc(1024, BF16))
        dma("pool", maskc.ap, maskc_d, r=(), w=[maskc.res])
        mngt = T(ar.alloc(512))
        dma("sp", mngt.ap, mng, r=(), w=[mngt.res])
        m2 = ar.mark()
        ROW += [T(ar.alloc(TOK)) for _ in range(2)]
        LF, WS, BLC, BC, ZR = ROW[1], ROW[2], ROW[3], ROW[4], ROW[5]
        sm2 = ar.alloc(256)
        amax = T(sm2[:, 0:1])
        namax = T(sm2[:, 1:2])
        fb = T(sm2[:, 2:4])
        minit = T(sm2[:, 4:5])
        mprevc = T(sm2[:, 16:32])
        bst = T(sm2[:, 32:48])
        alast = T(sm2[:, 48:64])
        wpv = T(sm2[:, 64:80])
        Wd = T(sm2[:, 80:144])
        mrow = T(sm2[:, 144:208])
        one1 = T(sm2[:, 208:209])
        fbrep = T(ar.alloc(256))
        ones64 = T(ar.alloc(128))
        wscol = T(ar.alloc(128))
        wscol_v = wscol.ap.rearrange("p (c r h) -> p c r h", c=NT, r=2)
        pay = T(ar.alloc(1048))
        kws = [T(ar.alloc(512, BF16)) for _ in range(2)]
        for rb in (WS, BLC, BC, ZR):
            mset("pool", rb.ap[0:64, :], 0.0, w=[rb.res])
        mset("pool", sm2[:, 0:256], 0.0, w=[amax.res, namax.res, fb.res, minit.res, mprevc.res, bst.res, alast.res,
                                             wpv.res, Wd.res, mrow.res, one1.res])
        mset("dve", one1.ap, 1.0, w=[one1.res])
        mset("pool", ones64.ap[0:64, :], 1.0, w=[ones64.res])
        DIRS = ((0, 4), (32, 36))

        def rev(ap2d):
            return ap2d[:, ::-1]

        act(LF.ap[0:64, :], GF.ap[0:64, :], AF.Exp, r=[GF.res], w=[LF.res], scale=-1.0)
        act(LF.ap[0:64, :], LF.ap[0:64, :], AF.Ln, r=[LF.res], w=[LF.res], bias=1.0)
        ts("dve", LF.ap[0:64, :], LF.ap[0:64, :], -1.0, None, ALU.mult, None, r=[LF.res], w=[LF.res])
        if MSUB < 2.1:
            return
        P.add("dve", lambda e: e.tensor_tensor_scan(BC.ap[0:4, :], LF.ap[0:4, :], ZR.ap[0:4, :], 0.0, ALU.add, ALU.add),
              r=[LF.res, ZR.res], w=[BC.res])
        P.add("dve", lambda e: e.tensor_tensor_scan(rev(BC.ap[32:36, :]), rev(LF.ap[32:36, :]), rev(ZR.ap[32:36, :]), 0.0,
                                                    ALU.add, ALU.add), r=[LF.res, ZR.res], w=[BC.res])
        for c in range(NT):
            pb = 1 + c % 2
            pst = bank_bf(pb)[:, 0:512].rearrange("p (h d) -> p h d", h=4)
            for h in range(4):
                tr(pst[:, h, :], qk[:, 4 + h, c * 128:(c + 1) * 128], ident.ap, r=[qk_res[4 + h], ident.res], w=[psres[pb]])
            cp("dve", ktm[:, c], pst, r=[psres[pb]], w=[ktm_res[c]])
        ago = ag_out.ap().rearrange("(r p) c -> p r c", p=128)

        gsc = T(ar.alloc(NR * 16))
        gsc_v = gsc.ap.rearrange("p (r c) -> p r c", r=NR)
        cct = T(ar.alloc(2 * NR * (NR + 1)))
        cct_v = cct.ap.rearrange("p (d i k) -> p d i k", d=2, i=NR)
        dma("sp", cct.ap, ccc_d, r=(), w=[cct.res])
        dma("sp", gsc_v, ago[:, :, 1032:1048], r=[agout_res], w=[gsc.res])
        if MSUB < 4:
            return
        prod = T(ar.alloc(256))
        Em = [T(ar.alloc(32)) for _ in range(2)]
        coef = [T(ar.alloc(32)) for _ in range(2)]
        minr = [T(ar.alloc(4)) for _ in range(2)]
        min_ = [T(ar.alloc(4)) for _ in range(2)]
        gbuf = [T(ar.alloc(516)) for _ in range(2)]
        for dr in range(2):
            pv4 = prod.ap[:, 0:NR * 4 * NR].rearrange("p (i h k) -> p i h k", i=NR, h=4)
            selk = bc(cct_v[:, dr, :, 0:NR].unsqueeze(2), [128, NR, 4, NR])
            Fv = bc(gsc_v[:, :, dr * 4:dr * 4 + 4].rearrange("p k h -> p h k").unsqueeze(1), [128, NR, 4, NR])
            tt("dve", pv4, selk, Fv, ALU.mult, r=[cct.res, gsc.res], w=[prod.res])
            Ev = Em[dr].ap[:, 0:NR * 4].rearrange("p (i h) -> p i h", i=NR)
            red(Ev, pv4, ALU.add, r=[prod.res], w=[Em[dr].res])
            tt("dve", Ev, Ev, gsc_v[:, :, 8 + dr * 4:12 + dr * 4], ALU.add, r=[Em[dr].res, gsc.res], w=[Em[dr].res])
            tt("dve", Ev, Ev, bc(cct_v[:, dr, :, NR:NR + 1], [128, NR, 4]), ALU.add, r=[Em[dr].res, cct.res], w=[Em[dr].res])
            red(minr[dr].ap, Ev.rearrange("p i h -> p h i"), ALU.max, r=[Em[dr].res], w=[minr[dr].res])
            ts("dve", min_[dr].ap, minr[dr].ap, 0.0, None, ALU.max, None, r=[minr[dr].res], w=[min_[dr].res])
            tt("dve", Ev, Ev, bc(min_[dr].ap.unsqueeze(1), [128, NR, 4]), ALU.subtract, r=[Em[dr].res, min_[dr].res],
               w=[Em[dr].res])
            act(coef[dr].ap[:, 0:NR * 4], Em[dr].ap[:, 0:NR * 4], AF.Exp, r=[Em[dr].res], w=[coef[dr].res])
            cv = coef[dr].ap[:, 0:NR * 4].rearrange("p (i h) -> p i h", i=NR)
            eng = "dve"
            for i in range(NR):
                gb = gbuf[i % 2]
                dma("sp", gb.ap, ag_out.ap()[i * 128:(i + 1) * 128, dr * 516:(dr + 1) * 516], r=[agout_res], w=[gb.res])
                for h in range(4):
                    dst = Cst[dr].ap[:, h * 129:(h + 1) * 129]
                    src = gb.ap[:, h * 129:(h + 1) * 129]
                    if i == 0:
                        ts(eng, dst, src, cv[:, 0, h:h + 1], None, ALU.mult, None, r=[gb.res, coef[dr].res], w=[Cst[dr].res])
                    else:
                        stt(eng, dst, src, cv[:, i, h:h + 1], dst, ALU.mult, ALU.add, r=[gb.res, coef[dr].res, Cst[dr].res],
                            w=[Cst[dr].res])
            cp("act", Cb[dr].ap, Cst[dr].ap, r=[Cst[dr].res], w=[Cb[dr].res])
        cp("dve", mrow.ap[0:1, 0:4], min_[0].ap[0:1, 0:4], r=[min_[0].res], w=[mrow.res])
        cp("dve", mrow.ap[0:1, 32:36], min_[1].ap[0:1, 0:4], r=[min_[1].res], w=[mrow.res])
        mm(bank(0)[0:64, 0:1], mrow.ap[0:1, 0:64], one1.ap[0:1, 0:1], start=True, stop=True, r=[mrow.res, one1.res], w=[psres[0]])
        cp("dve", minit.ap[0:64, :], bank(0)[0:64, 0:1], r=[psres[0]], w=[minit.res])
        MM = ROW[2]
        P.add("dve", lambda e: e.tensor_tensor_scan(MM.ap[0:4, :], LF.ap[0:4, :], GI.ap[0:4, :], minit.ap[0:4, :], ALU.add, ALU.max),
              r=[LF.res, GI.res, minit.res], w=[MM.res])
        P.add("dve", lambda e: e.tensor_tensor_scan(rev(MM.ap[32:36, :]), rev(LF.ap[32:36, :]), rev(GI.ap[32:36, :]),
                                                    minit.ap[32:36, :], ALU.add, ALU.max),
              r=[LF.res, GI.res, minit.res], w=[MM.res])
        act(ZR.ap[0:64, :], MM.ap[0:64, :], AF.Exp, r=[MM.res], w=[ZR.res], scale=-1.0)
        cp("dve", mprevc.ap[0:4, 1:16], MM.ap[0:4, 127:1920:128], r=[MM.res], w=[mprevc.res])
        cp("dve", mprevc.ap[0:4, 0:1], minit.ap[0:4, :], r=[minit.res], w=[mprevc.res])
        cp("dve", mprevc.ap[32:36, 0:15], MM.ap[32:36, 128:2048:128], r=[MM.res], w=[mprevc.res])
        cp("dve", mprevc.ap[32:36, 15:16], minit.ap[32:36, :], r=[minit.res], w=[mprevc.res])
        cp("dve", bst.ap[0:4, 1:16], BC.ap[0:4, 127:1920:128], r=[BC.res], w=[bst.res])
        cp("dve", bst.ap[32:36, 0:15], BC.ap[32:36, 128:2048:128], r=[BC.res], w=[bst.res])
        v3 = lambda t_: t_.ap[0:64, :].rearrange("p (c l) -> p c l", c=NT)
        b3 = lambda t_: bc(t_.ap[0:64, :].unsqueeze(2), [64, NT, 128])
        tt("dve", v3(BLC), v3(BC), b3(bst), ALU.subtract, r=[BC.res, bst.res], w=[BLC.res])
        tt("dve", MM.ap[0:64, :], BLC.ap[0:64, :], MM.ap[0:64, :], ALU.subtract, r=[BLC.res, MM.res], w=[MM.res])
        tt("dve", BLC.ap[0:64, :], GI.ap[0:64, :], BLC.ap[0:64, :], ALU.subtract, r=[GI.res, BLC.res], w=[BLC.res])
        AL, BE = MM, BLC
        cp("dve", alast.ap[0:4, :], AL.ap[0:4, 127:2048:128], r=[AL.res], w=[alast.res])
        cp("dve", alast.ap[32:36, :], AL.ap[32:36, 0:2048:128], r=[AL.res], w=[alast.res])
        WI, WK = ROW[0], ROW[1]
        tt("dve", v3(WI), v3(AL), b3(mprevc), ALU.add, r=[AL.res, mprevc.res, GI.res], w=[WI.res])
        act(WI.ap[0:64, :], WI.ap[0:64, :], AF.Exp, r=[WI.res], w=[WI.res])
        tt("dve", v3(WK), v3(BE), b3(alast), ALU.add, r=[BE.res, alast.res, LF.res], w=[WK.res])
        act(WK.ap[0:64, :], WK.ap[0:64, :], AF.Exp, r=[WK.res], w=[WK.res])
        tt("dve", wpv.ap[0:64, :], mprevc.ap[0:64, :], alast.ap[0:64, :], ALU.add, r=[mprevc.res, alast.res], w=[wpv.res])
        act(wpv.ap[0:64, :], wpv.ap[0:64, :], AF.Exp, r=[wpv.res], w=[wpv.res])
        tt("dve", Wd.ap[0:64, :].rearrange("p (c h) -> p c h", c=NT), bc(wpv.ap[0:64, :].unsqueeze(2), [64, NT, 4]), diag4,
           ALU.mult, r=[wpv.res, selt.res], w=[Wd.res])
        for c in range(NT):
            for dr, (lo, hi) in enumerate(DIRS):
                for q, rb in enumerate((WI, ZR, WK)):
                    col = ((c * 2 + dr) * 3 + q) * 4
                    mm(bank(1)[:, col:col + 4], rb.ap[lo:hi, c * 128:(c + 1) * 128], I4[lo:hi, :], start=True, stop=True,
                       r=[rb.res, selt.res], w=[psres[1]])
        cp("dve", colq.ap, bank(1)[:, 0:384], r=[psres[1]], w=[colq.res])
        for dr, (lo, hi) in enumerate(DIRS):
            mm(bank(2)[:, dr * 64:(dr + 1) * 64], ones64.ap[lo:hi, :], Wd.ap[lo:hi, :], start=True, stop=True,
               r=[ones64.res, Wd.res], w=[psres[2]])
        cp("dve", wpbc.ap, bank(2)[:, 0:128], r=[psres[2]], w=[wpbc.res])

        if MSUB < 5:
            return
        P.barrier()
        ar.release(m2)
        hbuf = ar.alloc(NT * 512, BF16).rearrange("p (c f) -> p c f", c=NT)
        hbuf_res = [Res() for _ in range(NT)]
        DT = [T(ar.alloc(512, BF16)) for _ in range(2)]
        PTm = [T(ar.alloc(512, BF16)) for _ in range(2)]
        kwl = [T(ar.alloc(512, BF16)) for _ in range(2)]
        tmp1 = [T(ar.alloc(512)) for _ in range(2)]
        tmp2 = [T(ar.alloc(512)) for _ in range(2)]
        hd = [T(ar.alloc(512)) for _ in range(2)]
        hs = T(ar.alloc(512))
        sq2 = T(ar.alloc(512))
        y1 = T(ar.alloc(512))
        ya = T(ar.alloc(512, BF16))
        sm3 = ar.alloc(64)
        t4 = [T(sm3[:, 4 * i:4 * i + 4]) for i in range(2)]
        rinv4 = [T(sm3[:, 8 + 4 * i:12 + 4 * i]) for i in range(2)]
        wr4 = [T(sm3[:, 16 + 4 * i:20 + 4 * i]) for i in range(2)]
        ss4 = T(sm3[:, 24:28])
        rt4 = T(sm3[:, 28:32])
        r4 = T(sm3[:, 32:36])
        q4 = lambda ap: ap.rearrange("p (a r) -> p a r", a=2)

        def post(c, hdt):
            tt("dve", hs.ap, hbuf[:, c, :], hdt.ap, ALU.add, r=[hbuf_res[c], hdt.res], w=[hs.res])
            act(sq2.ap, hs.ap, AF.Square, r=[hs.res], w=[sq2.res])
            red(ss4.ap, sq2.ap.rearrange("p (h d) -> p h d", h=4), ALU.add, r=[sq2.res], w=[ss4.res])
            act(rt4.ap, ss4.ap, AF.Sqrt, r=[ss4.res], w=[rt4.res], scale=1.0 / 128, bias=EPS)
            recip(r4.ap, rt4.ap, r=[rt4.res], w=[r4.res])
            tt("dve", y1.ap.rearrange("p (h d) -> p h d", h=4), hs.ap.rearrange("p (h d) -> p h d", h=4),
               bc(r4.ap.unsqueeze(2), [128, 4, 128]), ALU.mult, r=[hs.res, r4.res], w=[y1.res])
            tt("dve", y1.ap, y1.ap, mngt.ap, ALU.mult, r=[y1.res, mngt.res], w=[y1.res])
            tt("dve", ya.ap, y1.ap, sigo[:, c, :], ALU.mult, r=[y1.res, sigo_res[c]], w=[ya.res])
            pst = bank_bf(0)[:, 0:512].rearrange("p (j t) -> p j t", j=4)
            for j in range(4):
                tr(pst[:, j, :], ya.ap[:, j * 128:(j + 1) * 128], ident.ap, r=[ya.res, ident.res], w=[psres[0]])
            cp("act", yT[:, 0:4, c * 128:(c + 1) * 128], pst, r=[psres[0]], w=[yT_res[c]])

        def do_chunk(dr, c, first, last, par):
            lo, hi = DIRS[dr]
            cs = slice(c * 128, (c + 1) * 128)
            for h in range(4):
                mm(bank(0)[:, h * 128:(h + 1) * 128], qk[:, 4 + h, cs], qk[:, h, cs], start=True, stop=True,
                   r=[qk_res[4 + h], qk_res[h]], w=[psres[0]])
            mm(bank(1), ident.ap, maskc.ap[:, dr * 512:(dr + 1) * 512], start=True, stop=False, r=[ident.res, maskc.res],
               w=[psres[1]])
            for h in range(4):
                mm(bank(1)[:, h * 128:(h + 1) * 128], sel_v[lo:hi, h, :], AL.ap[lo:hi, cs], start=False, stop=False,
                   r=[selt.res, AL.res], w=[psres[1]])
                mm(bank(1)[:, h * 128:(h + 1) * 128], BE.ap[lo:hi, cs], sel_v[lo:hi, h, :], start=False, stop=(h == 3),
                   r=[selt.res, BE.res], w=[psres[1]])
            if LPART < 1:
                return
            act(DT[par].ap, bank(1), AF.Exp, r=[psres[1]], w=[DT[par].res])
            tt("dve", PTm[par].ap, bank(0), DT[par].ap, ALU.mult, r=[psres[0], DT[par].res], w=[PTm[par].res])
            for h in ([] if 'intra' in LSKIP else range(4)):
                mm(ps_t[:, 2 + h // 2, (h % 2) * 129:(h % 2) * 129 + 129], PTm[par].ap[:, h * 128:(h + 1) * 128], vm[:, c, h, :],
                   start=True, stop=True, r=[PTm[par].res, vm_res[c]], w=[psres[2 + h // 2]])
            for h in ([] if 'inter' in LSKIP else range(4)):
                mm(ps_t[:, 4 + h // 2, (h % 2) * 129:(h % 2) * 129 + 129], qk[:, h, cs], Cb[dr].ap[:, h * 129:(h + 1) * 129],
                   start=True, stop=True, r=[qk_res[h], Cb[dr].res], w=[psres[4 + h // 2]])
            if LPART < 2:
                return
            Nv = ps_t[:, 2:4, 0:258].rearrange("p a (r c) -> p a r c", c=129)
            Iv = ps_t[:, 4:6, 0:258].rearrange("p a (r c) -> p a r c", c=129)
            nres, ires = [psres[2], psres[3]], [psres[4], psres[5]]
            wint = q4(colq_v[:, c, dr, 0, :])
            eclv = q4(colq_v[:, c, dr, 1, :])
            tt("dve", q4(t4[par].ap), Iv[:, :, :, 128], wint, ALU.mult, r=ires + [colq.res], w=[t4[par].res])
            tt("dve", q4(t4[par].ap), q4(t4[par].ap), Nv[:, :, :, 128], ALU.add, r=nres + [t4[par].res], w=[t4[par].res])
            tt("dve", q4(wr4[par].ap), q4(t4[par].ap), eclv, ALU.max, r=[t4[par].res, colq.res], w=[wr4[par].res])
            stt("dve", t4[par].ap, t4[par].ap, -1.0, wr4[par].ap, ALU.mult, ALU.max, r=[t4[par].res, wr4[par].res], w=[t4[par].res])
            recip(rinv4[par].ap, t4[par].ap, r=[t4[par].res], w=[rinv4[par].res])
            tt("dve", q4(wr4[par].ap), wint, q4(rinv4[par].ap), ALU.mult, r=[colq.res, rinv4[par].res], w=[wr4[par].res])
            f4 = lambda t_: t_.ap.rearrange("p (a r d) -> p a r d", a=2, r=2)
            tt("dve", f4(tmp1[par]), Iv[:, :, :, 0:128], bc(q4(wr4[par].ap).unsqueeze(3), [128, 2, 2, 128]), ALU.mult,
               r=ires + [wr4[par].res], w=[tmp1[par].res])
            tt("dve", f4(tmp2[par]), Nv[:, :, :, 0:128], bc(q4(rinv4[par].ap).unsqueeze(3), [128, 2, 2, 128]), ALU.mult,
               r=nres + [rinv4[par].res], w=[tmp2[par].res])
            if LPART < 3:
                return
            if first:
                tt("dve", hbuf[:, c, :], tmp1[par].ap, tmp2[par].ap, ALU.add, r=[tmp1[par].res, tmp2[par].res], w=[hbuf_res[c]])
            else:
                tt("dve", hd[par].ap, tmp1[par].ap, tmp2[par].ap, ALU.add, r=[tmp1[par].res, tmp2[par].res], w=[hd[par].res])
                post(c, hd[par])
            if last or LPART < 4:
                return
            kwv = kwl[par].ap.rearrange("p (h d) -> p h d", h=4)
            tt("dve", kwv, ktm[:, c], bc(colq_v[:, c, dr, 2, :].unsqueeze(2), [128, 4, 128]), ALU.mult,
               r=[ktm_res[c], colq.res], w=[kwl[par].res])
            for h in range(4):
                mm(ps_t[:, 6 + h // 2, (h % 2) * 129:(h % 2) * 129 + 129], kwv[:, h, :], vm[:, c, h, :],
                   start=True, stop=True, r=[kwl[par].res, vm_res[c]], w=[psres[6 + h // 2]])
            Uv = ps_t[:, 6:8, 0:258].rearrange("p a (r c) -> p a r c", c=129)
            Cv = Cst[dr].ap.rearrange("p (a r c) -> p a r c", a=2, r=2)
            tt("dve", Cv, Cv, bc(q4(wpbc_v[:, dr, c, :]).unsqueeze(3), [128, 2, 2, 129]), ALU.mult,
               r=[Cst[dr].res, wpbc.res], w=[Cst[dr].res])
            tt("dve", Cv, Cv, Uv, ALU.add, r=[Cst[dr].res, psres[6], psres[7]], w=[Cst[dr].res])
            cp("act", Cb[dr].ap, Cst[dr].ap, r=[Cst[dr].res], w=[Cb[dr].res])

        for step in range(int(os.environ.get('LSTEPS', NT))):
            do_chunk(0, step, first=(step < 8), last=(step == NT - 1), par=0)
            do_chunk(1, NT - 1 - step, first=(step < 8), last=(step == NT - 1), par=1)
        if "yaT" in debug:
            dbg("yaT", yT[:, 0:4, :], yT_res)

    if stage >= 3:
        mlstm_pass()

    if stage >= 4:
        P.barrier()
        ar.release(m_mixer)
        wo = T(ar.alloc(8 * D, BF16))
        wo_v = wo.ap.rearrange("p (k c) -> p k c", k=8)
        dma("sp", wo_v, wob.ap().rearrange("(k p) c -> p k c", p=128), r=[wob_res], w=[wo.res])
        wpg = T(ar.alloc(8 * D, BF16))
        wpg_v = wpg.ap.rearrange("p (k c) -> p k c", k=8)
        dma("sp", wpg_v, wpgb.ap().rearrange("(k p) c -> p k c", p=128), r=[wpgb_res], w=[wpg.res])
        wpu = T(ar.alloc(2 * D, BF16))
        wpu_v = wpu.ap.rearrange("p (k c) -> p k c", k=2)
        dma("sp", wpu_v, wpub.ap().rearrange("(k p) c -> p k c", p=128), r=[wpub_res], w=[wpu.res])
        h1 = ar.alloc(4 * D).rearrange("p (t c) -> p t c", t=4)
        h1_res = [Res() for _ in range(4)]
        hnT = T(ar.alloc(8 * 512, BF16))
        hnT_v = hnT.ap.rearrange("p (k t) -> p k t", k=8)
        hnT_res = [Res() for _ in range(4)]
        zT = ar.alloc(32 * 512, BF16).rearrange("p (h t) -> p h t", h=32)
        zT_res = [Res() for _ in range(32)]
        w1c = [T(ar.alloc(8 * 512, BF16)) for _ in range(2)]
        w2c = [T(ar.alloc(4 * 512, BF16)) for _ in range(3)]
        xnb = [T(ar.alloc(D, BF16)) for _ in range(2)]
        junk = T(ar.alloc(D, BF16))
        small = ar.alloc(64)
        ss = [T(small[:, i:i + 1]) for i in range(4)]
        rt = [T(small[:, 4 + i:5 + i]) for i in range(4)]
        rr = [T(small[:, 8 + i:9 + i]) for i in range(4)]
        zsq = [T(ar.alloc(512)) for _ in range(2)]
        gate = [T(ar.alloc(D)) for _ in range(2)]
        hpT = [T(ar.alloc(8 * 128, BF16)) for _ in range(4)]
        pst_t = [T(ar.alloc(256)) for _ in range(2)]
        pbf = [T(ar.alloc(256, BF16)) for _ in range(2)]
        pT = [T(ar.alloc(256, BF16)) for _ in range(2)]
        otile = [T(ar.alloc(D)) for _ in range(2)]

        def norm_pipe_sb(gT, dst_of, dres_of, tbanks):
            for step in range(4 + 2):
                i = step
                if i < 4:
                    act(junk.ap, h1[:, i, :], AF.Square, r=[h1_res[i]], w=[junk.res, ss[i].res], accum=ss[i].ap)
                    act(rt[i].ap, ss[i].ap, AF.Sqrt, r=[ss[i].res], w=[rt[i].res], scale=1.0 / D, bias=EPS)
                    recip(rr[i].ap, rt[i].ap, r=[rt[i].res], w=[rr[i].res])
                i = step - 1
                if 0 <= i < 4:
                    act(xnb[i % 2].ap, h1[:, i, :], AF.Copy, r=[h1_res[i], rr[i].res], w=[xnb[i % 2].res], scale=rr[i].ap)
                i = step - 2
                if 0 <= i < 4:
                    pb = tbanks[i % 2]
                    pst = bank_bf(pb).rearrange("p (k t) -> p k t", k=8)
                    for k in range(8):
                        tr(pst[:, k, :], xnb[i % 2].ap[:, k * 128:(k + 1) * 128], ident.ap,
                           r=[xnb[i % 2].res, ident.res], w=[psres[pb]])
                    tt("dve", dst_of(i), pst, bc(gT.unsqueeze(2), [128, 8, 128]), ALU.mult,
                       r=[psres[pb], gv.res], w=[dres_of(i)])

        for g in range(4):
            for t in range(4):
                tg = g * 4 + t
                dma("sp", h1[:, t, :], x_ext[HALO + tg * 128:HALO + (tg + 1) * 128, :], r=(), w=[h1_res[t]])
            for t in range(4):
                tg = g * 4 + t
                for n in range(2):
                    for k in range(8):
                        mm(bank(2 * t + n), yT[:, k, tg * 128:(tg + 1) * 128], wo_v[:, k, n * 512:(n + 1) * 512],
                           start=(k == 0), stop=(k == 7), r=[yT_res[tg], wo.res], w=[psres[2 * t + n]])
            for t in range(4):
                tt("dve", h1[:, t, :].rearrange("p (n c) -> p n c", n=2), ps_t[:, 2 * t:2 * t + 2, :],
                   h1[:, t, :].rearrange("p (n c) -> p n c", n=2), ALU.add, r=[psres[2 * t], psres[2 * t + 1], h1_res[t]],
                   w=[h1_res[t]])
            norm_pipe_sb(g2T, lambda i: hnT_v[:, :, i * 128:(i + 1) * 128], lambda i: hnT_res[i], tbanks=(0, 1))
            for hcb in range(8):
                w = w1c[hcb % 2]
                wv = w.ap.rearrange("p (k c) -> p k c", k=8)
                dma("sp", wv, w1b.ap()[:, hcb * 512:(hcb + 1) * 512].rearrange("(k p) c -> p k c", p=128), r=[w1b_res], w=[w.res])
                for hl in range(4):
                    hc = hcb * 4 + hl
                    pb = hc % 4
                    for k in range(8):
                        mm(bank(pb), wv[:, k, hl * 128:(hl + 1) * 128], hnT_v[:, k, :], start=(k == 0), stop=(k == 7),
                           r=[w.res] + hnT_res, w=[psres[pb]])
                    zq = zsq[hc % 2]
                    act(zq.ap, bank(pb), AF.Square, r=[psres[pb]], w=[zq.res])
                    stt("dve", zT[:, hc, :], bank(pb), 0.0, zq.ap, ALU.is_gt, ALU.mult, r=[psres[pb], zq.res], w=[zT_res[hc]])
            for n in range(2):
                for hcb in range(8):
                    w = w2c[(n * 8 + hcb) % 3]
                    wv = w.ap.rearrange("p (h c) -> p h c", h=4)
                    dma("sp", wv, w2b.ap()[hcb * 512:(hcb + 1) * 512, n * 512:(n + 1) * 512].rearrange("(h p) c -> p h c", p=128),
                        r=[w2b_res], w=[w.res])
                    for hl in range(4):
                        hc = hcb * 4 + hl
                        for t in range(4):
                            mm(bank(4 + t), zT[:, hc, t * 128:(t + 1) * 128], wv[:, hl, :], start=(hc == 0), stop=(hc == 31),
                               r=[zT_res[hc], w.res], w=[psres[4 + t]])
                for t in range(4):
                    tt("dve", h1[:, t, n * 512:(n + 1) * 512], bank(4 + t), h1[:, t, n * 512:(n + 1) * 512], ALU.add,
                       r=[psres[4 + t], h1_res[t]], w=[h1_res[t]])
            for t in range(4):
                tg = g * 4 + t
                dma("sp", pst_t[t % 2].ap, p_c[tg * 128:(tg + 1) * 128, :], r=(), w=[pst_t[t % 2].res]) if t < 2 else None
            norm_pipe_sb(gpT, lambda i: hpT[i].ap.rearrange("p (k t) -> p k t", k=8), lambda i: hpT[i].res, tbanks=(0, 4))
            for t in range(4):
                tg = g * 4 + t
                q = t % 2
                gb, ub = 4 * q, 4 * q + 2
                hv = hpT[t].ap.rearrange("p (k t) -> p k t", k=8)
                if t >= 2:
                    dma("sp", pst_t[q].ap, p_c[tg * 128:(tg + 1) * 128, :], r=(), w=[pst_t[q].res])
                for n in range(2):
                    for k in range(8):
                        mm(bank(gb + n), hv[:, k, :], wpg_v[:, k, n * 512:(n + 1) * 512], start=(k == 0), stop=(k == 7),
                           r=[hpT[t].res, wpg.res], w=[psres[gb + n]])
                act(gate[q].ap.rearrange("p (n c) -> p n c", n=2), ps_t[:, gb:gb + 2, :], AF.Sigmoid,
                    r=[psres[gb], psres[gb + 1]], w=[gate[q].res])
                cp("act", pbf[q].ap, pst_t[q].ap, r=[pst_t[q].res], w=[pbf[q].res])
                ptr = bank_bf(ub)[:, 0:256].rearrange("p (k t) -> p k t", k=2)
                for k in range(2):
                    tr(ptr[:, k, :], pbf[q].ap[:, k * 128:(k + 1) * 128], ident.ap, r=[pbf[q].res, ident.res], w=[psres[ub]])
                cp("dve", pT[q].ap.rearrange("p (k t) -> p k t", k=2), ptr, r=[psres[ub]], w=[pT[q].res])
                for n in range(2):
                    for k in range(2):
                        mm(bank(ub + n), pT[q].ap[:, k * 128:(k + 1) * 128], wpu_v[:, k, n * 512:(n + 1) * 512],
                           start=(k == 0), stop=(k == 1), r=[pT[q].res, wpu.res], w=[psres[ub + n]])
                ot = otile[q]
                tt("dve", ot.ap.rearrange("p (n c) -> p n c", n=2), ps_t[:, ub:ub + 2, :], gate[q].ap.rearrange("p (n c) -> p n c", n=2),
                   ALU.mult, r=[psres[ub], psres[ub + 1], gate[q].res], w=[ot.res])
                tt("dve", ot.ap, ot.ap, h1[:, t, :], ALU.add, r=[ot.res, h1_res[t]], w=[ot.res])
                dma("sp", out_d[tg * 128:(tg + 1) * 128, :], ot.ap, r=[ot.res], w=())

    if stage < 99:
        P.barrier()
        ar.release(m_mixer)
        z = T(ar.alloc(D))
        mset("dve", z.ap, 0.0, w=[z.res])
        for t in range(NT):
            dma("sp", out_d[t * 128:(t + 1) * 128, :], z.ap, r=[z.res], w=())
    info = P.emit(nc, stack)
    info["arena_peak_words"] = ar.peak
    stack.close()
    return nc, info, dbg_out


def host_consts():
    ident = np.eye(128, dtype=np.float32)
    selc = np.zeros((64, 4 * 128 + 64 + 4), np.float32)
    for p in list(range(4)) + list(range(32, 36)):
        h = p % 32
        selc[p, h * 128:(h + 1) * 128] = 1.0
        for c in range(16):
            selc[p, 512 + c * 4 + h] = 1.0
        selc[p, 576 + h] = 1.0
    s = np.arange(128)[:, None]
    l = np.arange(128)[None, :]
    mf = np.where(s <= l, 0.0, NEG).astype(np.float32)
    mb = np.where(s >= l, 0.0, NEG).astype(np.float32)
    maskc = np.concatenate([np.tile(mf, (1, 4)), np.tile(mb, (1, 4))], axis=1)
    return ident, selc, maskc


def host_gtab(rpb):
    a = np.arange(2)[:, None, None, None, None]
    cp_ = np.arange(64)[None, :, None, None, None]
    dl = np.arange(-3, 4)[None, None, :, None, None]
    b = np.arange(2)[None, None, None, :, None]
    c = np.arange(64)[None, None, None, None, :]
    drow = 2 * dl + a - b
    cs = np.clip(c - 8, 0, 48)
    colok = (cp_ >= cs) & (cp_ < cs + 16)
    ridx = np.broadcast_to(drow + 7, (2, 64, 7, 2, 64))
    cidx = np.broadcast_to(np.clip(cp_ - c + 15, 0, 30), (2, 64, 7, 2, 64))
    ok = np.broadcast_to(colok, (2, 64, 7, 2, 64))
    out = np.empty((2, 64, 8, 7, 2, 64), np.float32)
    for h in range(8):
        out[:, :, h] = np.where(ok, rpb[h][ridx, cidx], np.float32(NEG))
    return np.ascontiguousarray(out.reshape(128, 8 * 7 * 128))


def host_rmcol(j):
    R0 = j * 32
    cols = []
    for eq, es in na_pairs():
        for e in es:
            for b in range(2):
                rq = R0 + 2 * (eq - 2) + b
                rs = min(max(rq - 4, 0), 120)
                col = np.empty((2, 64), np.float32)
                for a in range(2):
                    rk = R0 + 2 * (e - 2) + a
                    ok = (0 <= rk < 128) and (rs <= rk < rs + 8)
                    col[a, :] = 0.0 if ok else NEG
                cols.append(col.reshape(128))
    return np.ascontiguousarray(np.stack(cols, axis=1))


def host_ccc(r):
    j = r % 4
    oth = [q for q in range(4) if q != j]
    out = np.zeros((2, 3, 4), np.float32)
    for i, qi in enumerate(oth):
        out[0, i, 3] = 0.0 if qi < j else NEG
        out[1, i, 3] = 0.0 if qi > j else NEG
        for k, qk_ in enumerate(oth):
            if qi < qk_ < j:
                out[0, i, k] = 1.0
            if j < qk_ < qi:
                out[1, i, k] = 1.0
    return np.ascontiguousarray(np.broadcast_to(out.reshape(1, -1), (128, 24)))


def make_in_maps(inputs):
    f = lambda a: np.ascontiguousarray(np.asarray(a, dtype=np.float32))
    x = f(inputs["x"])
    p = f(inputs["p"])[0]
    w_in = f(inputs["w_in"])[0]
    ident, selc, maskc = host_consts()
    wg = np.zeros((D, 128), np.float32)
    gb = f(inputs["gate_b"])[0]
    gbias = np.zeros((64, 2), np.float32)
    wg[:, 0:4] = w_in[:, 2048 + 0:2048 + 4]
    wg[:, 32:36] = w_in[:, 2048 + 8:2048 + 12]
    wg[:, 64:68] = w_in[:, 2048 + 4:2048 + 8]
    wg[:, 96:100] = w_in[:, 2048 + 12:2048 + 16]
    gbias[0:4, 0] = gb[0:4]
    gbias[32:36, 0] = gb[8:12]
    gbias[0:4, 1] = gb[4:8]
    gbias[32:36, 1] = gb[12:16]
    gvec = np.concatenate([f(inputs[k])[0].reshape(8, 128).T for k in ("norm1_g", "norm2_g", "ple_norm_g")], axis=1)
    convw = f(inputs["conv_w"])[0].reshape(3, 8, 128).transpose(2, 1, 0).reshape(128, 24)
    convb = f(inputs["conv_b"])[0].reshape(8, 128).T
    mng = np.broadcast_to(f(inputs["mlstm_norm_g"])[0].reshape(1, 512), (128, 512))
    gq = f(inputs["q_norm_g"])[0].reshape(4, 128).T
    gk = f(inputs["k_norm_g"])[0].reshape(4, 128).T
    gqk = np.concatenate([gq, gk], axis=1)
    gtab = host_gtab(f(inputs["rpb"])[0])
    shared = dict(
        w_in=w_in, wg=wg, w_out=f(inputs["w_out"])[0], w_ff1=f(inputs["w_ff1"])[0], w_ff2=f(inputs["w_ff2"])[0],
        w_pg=f(inputs["w_ple_gate"])[0], w_pu=f(inputs["w_ple_up"])[0], gvec=np.ascontiguousarray(gvec),
        convw=np.ascontiguousarray(convw), convb=np.ascontiguousarray(convb), gbias=gbias,
        mng=np.ascontiguousarray(mng), gqk=np.ascontiguousarray(gqk), gtab=gtab, ident=ident, selc=selc,
        maskc=np.ascontiguousarray(maskc))
    in_maps = []
    for r in range(NCORES):
        b, j = r // 4, r % 4
        t0 = j * TOK
        xe = np.zeros((EXT, D), np.float32)
        lo, hi = t0 - HALO, t0 + TOK + HALO
        slo, shi = max(lo, 0), min(hi, 4 * TOK)
        xe[slo - lo:shi - lo] = x[b, slo:shi]
        m = dict(shared)
        m["x_ext"] = xe
        m["p_c"] = np.ascontiguousarray(p[b, t0:t0 + TOK])
        oth = [q for q in range(4) if q != j]
        m["x_oth"] = np.ascontiguousarray(np.concatenate([x[b, q * TOK:(q + 1) * TOK] for q in oth], axis=0))
        xn = np.zeros((128, D), np.float32)
        for i, q in enumerate(oth):
            if q * TOK - 1 >= 0:
                xn[2 * i] = x[b, q * TOK - 1]
            if (q + 1) * TOK < 4 * TOK:
                xn[2 * i + 1] = x[b, (q + 1) * TOK]
        m["x_nbr"] = xn
        m["rmcol"] = host_rmcol(j)
        m["ccc"] = host_ccc(r)
        in_maps.append(m)
    return in_maps


_NC_CACHE = {}


def kernel(**inputs):
    in_maps = make_in_maps(inputs)
    if "nc" not in _NC_CACHE:
        _NC_CACHE["nc"] = build_nc()[0]
    nc = _NC_CACHE["nc"]
    res = run_bass_kernel_spmd(nc, in_maps, core_ids=list(range(NCORES)))
    out = np.empty((2, 4 * TOK, D), np.float32)
    for r in range(NCORES):
        b, j = r // 4, r % 4
        out[b, j * TOK:(j + 1) * TOK] = res.results[r]["out"]
    return out
```
